# Optimizing a Trainium2 kernel written in Bass

```python
import jax, jax.numpy as jnp
from jax import lax
import numpy as np

D_MODEL = 1024
BATCH = 8
SEQ = 4096
DEPTH = 1

D_CONV = 512
CONV_WIDTH = 31
N_HEADS = 8
HEAD_DIM = 64
N_KV = 2
HPG = N_HEADS // N_KV
D_ATTN = N_HEADS * HEAD_DIM
D_KV = N_KV * HEAD_DIM
N_BRANCH = 3
D_IN = 2 * D_CONV + D_ATTN + 2 * N_BRANCH * D_KV + N_BRANCH * N_HEADS
CMP_BLOCK = 32
CMP_STRIDE = 16
CMP_HIDDEN = 256
SEL_BLOCK = 64
SEL_TOP = 16
WINDOW = 512
Q_CHUNK = 64
D_FF = 4 * D_MODEL
EPS = 1e-6
NEG = -1e30
FORCE = 1e30

kernel_name = "hymba_conformer_nsa_hybrid"


def rmsnorm(x, g):
    xf = x.astype(jnp.float32)
    y = xf * lax.rsqrt(jnp.mean(xf * xf, axis=-1, keepdims=True) + EPS)
    return (y * g.astype(jnp.float32)).astype(x.dtype)


def layernorm(x, g, b):
    xf = x.astype(jnp.float32)
    mu = jnp.mean(xf, axis=-1, keepdims=True)
    var = jnp.mean(jnp.square(xf - mu), axis=-1, keepdims=True)
    y = (xf - mu) * lax.rsqrt(var + EPS)
    return (y * g.astype(jnp.float32) + b.astype(jnp.float32)).astype(x.dtype)


def alibi_slopes(n):
    return jnp.exp2(-8.0 * jnp.arange(1, n + 1, dtype=jnp.float32) / n)


def conformer_conv(u_val, u_gate, dw_w, dw_b, ln_g, ln_b):
    u = u_val * jax.nn.sigmoid(u_gate)
    y = lax.conv_general_dilated(
        u, dw_w.astype(u.dtype), window_strides=(1,), padding=[(CONV_WIDTH - 1, 0)],
        dimension_numbers=("NWC", "WIO", "NWC"), feature_group_count=D_CONV) + dw_b
    y = layernorm(y, ln_g, ln_b)
    return jax.nn.silu(y)


def compress(kv, pe, w1, w2):
    B, G, T, dh = kv.shape
    ch = kv.reshape(B, G, T // CMP_STRIDE, CMP_STRIDE, dh)
    blocks = jnp.concatenate([ch[:, :, :-1], ch[:, :, 1:]], axis=3) + pe
    flat = blocks.reshape(B, G, blocks.shape[2], CMP_BLOCK * dh)
    return jax.nn.gelu(flat @ w1) @ w2


def nsa_attention(q, k_cmp, v_cmp, k_sel, v_sel, k_win, v_win, gates,
                  ck_pe, ck_w1, ck_w2, cv_pe, cv_w1, cv_w2):
    B, G, _, T, dh = q.shape
    n_chunks = T // Q_CHUNK
    n_sel = T // SEL_BLOCK
    n_top = min(SEL_TOP, n_sel)
    kc = compress(k_cmp, ck_pe, ck_w1, ck_w2)
    vc = compress(v_cmp, cv_pe, cv_w1, cv_w2)
    n_cmp = kc.shape[2]
    c_idx = jnp.arange(n_cmp)
    cmp_pos = c_idx * CMP_STRIDE + (CMP_BLOCK - 1)
    j_idx = jnp.arange(n_sel)
    overlap = ((c_idx[:, None] * CMP_STRIDE < (j_idx[None, :] + 1) * SEL_BLOCK)
               & (c_idx[:, None] * CMP_STRIDE + CMP_BLOCK > j_idx[None, :] * SEL_BLOCK)
               ).astype(jnp.float32)
    kb = k_sel.reshape(B, G, n_sel, SEL_BLOCK, dh)
    vb = v_sel.reshape(B, G, n_sel, SEL_BLOCK, dh)
    kw = jnp.pad(k_win, ((0, 0), (0, 0), (WINDOW, 0), (0, 0)))
    vw = jnp.pad(v_win, ((0, 0), (0, 0), (WINDOW, 0), (0, 0)))
    slopes = alibi_slopes(N_HEADS).reshape(1, G, HPG, 1, 1)
    scale = HEAD_DIM ** -0.5
    bi = jnp.arange(B)[:, None, None, None]
    gi = jnp.arange(G)[None, :, None, None]

    def chunk(ci):
        q0 = ci * Q_CHUNK
        qc = lax.dynamic_slice_in_dim(q, q0, Q_CHUNK, axis=3) * scale
        gc = lax.dynamic_slice_in_dim(gates, q0, Q_CHUNK, axis=3)
        t = q0 + jnp.arange(Q_CHUNK)

        s = jnp.einsum("bghqd,bgcd->bghqc", qc, kc).astype(jnp.float32)
        dist = (t[:, None] - cmp_pos[None, :]).astype(jnp.float32)
        valid = dist >= 0
        s = jnp.where(valid, s - slopes * dist, NEG)
        p_cmp = jnp.where(valid, jax.nn.softmax(s, axis=-1), 0.0)
        o_cmp = jnp.einsum("bghqc,bgcd->bghqd", p_cmp, vc.astype(jnp.float32))

        imp = jnp.einsum("bghqc,cj->bgqj", p_cmp, overlap)
        jt = t // SEL_BLOCK
        future = j_idx[None, :] > jt[:, None]
        forced = ((j_idx[None, :] == 0) | (j_idx[None, :] == jt[:, None])
                  | (j_idx[None, :] == jt[:, None] - 1))
        imp = jnp.where(forced, FORCE, jnp.where(future, NEG, imp))
        _, idx = lax.top_k(imp, n_top)
        ks = kb[bi, gi, idx]
        vs = vb[bi, gi, idx]
        s_pos = idx[..., None] * SEL_BLOCK + jnp.arange(SEL_BLOCK)
        d = (t[None, None, :, None, None] - s_pos)[:, :, None].astype(jnp.float32)
        s = jnp.einsum("bghqd,bgqnkd->bghqnk", qc, ks).astype(jnp.float32)
        s = jnp.where(d >= 0, s - slopes[..., None] * d, NEG)
        s = s.reshape(B, G, HPG, Q_CHUNK, n_top * SEL_BLOCK)
        p_sel = jax.nn.softmax(s, axis=-1)
        o_sel = jnp.einsum("bghqm,bgqmd->bghqd", p_sel,
                           vs.reshape(B, G, Q_CHUNK, n_top * SEL_BLOCK, dh).astype(jnp.float32))

        kwc = lax.dynamic_slice_in_dim(kw, q0, WINDOW + Q_CHUNK, axis=2)
        vwc = lax.dynamic_slice_in_dim(vw, q0, WINDOW + Q_CHUNK, axis=2)
        w_pos = q0 - WINDOW + jnp.arange(WINDOW + Q_CHUNK)
        dw = (t[:, None] - w_pos[None, :]).astype(jnp.float32)
        valid_w = (dw >= 0) & (dw < WINDOW) & (w_pos >= 0)[None, :]
        s = jnp.einsum("bghqd,bgkd->bghqk", qc, kwc).astype(jnp.float32)
        s = jnp.where(valid_w, s - slopes * dw, NEG)
        p_win = jax.nn.softmax(s, axis=-1)
        o_win = jnp.einsum("bghqk,bgkd->bghqd", p_win, vwc.astype(jnp.float32))

        gf = gc.astype(jnp.float32)
        o = gf[..., 0:1] * o_cmp + gf[..., 1:2] * o_sel + gf[..., 2:3] * o_win
        return o.astype(q.dtype)

    outs = lax.map(chunk, jnp.arange(n_chunks))
    return outs.transpose(1, 0, 4, 2, 3, 5).reshape(B, T, N_HEADS * HEAD_DIM)


def hybrid_layer(x, norm1_g, w_in, dw_w, dw_b, cln_g, cln_b, ck_pe, ck_w1, ck_w2,
                 cv_pe, cv_w1, cv_w2, w_out, norm2_g, w_ff1, w_ff2):
    B, T, _ = x.shape
    h = rmsnorm(x, norm1_g)
    z = h @ w_in
    o1 = D_CONV
    o2 = 2 * D_CONV
    o3 = o2 + D_ATTN
    o4 = o3 + 2 * N_BRANCH * D_KV
    u_val, u_gate, q, kv, g = z[..., :o1], z[..., o1:o2], z[..., o2:o3], z[..., o3:o4], z[..., o4:]
    kv = kv.reshape(B, T, 2 * N_BRANCH, N_KV, HEAD_DIM).transpose(2, 0, 3, 1, 4)
    q = q.reshape(B, T, N_KV, HPG, HEAD_DIM).transpose(0, 2, 3, 1, 4)
    g = jax.nn.sigmoid(g).reshape(B, T, N_KV, HPG, N_BRANCH).transpose(0, 2, 3, 1, 4)
    conv_out = conformer_conv(u_val, u_gate, dw_w, dw_b, cln_g, cln_b)
    attn_out = nsa_attention(q, kv[0], kv[1], kv[2], kv[3], kv[4], kv[5], g,
                             ck_pe, ck_w1, ck_w2, cv_pe, cv_w1, cv_w2)
    x = x + jnp.concatenate([conv_out, attn_out], axis=-1) @ w_out
    h = rmsnorm(x, norm2_g)
    x = x + jnp.square(jax.nn.relu(h @ w_ff1)) @ w_ff2
    return x


def setup_inputs(seed: int = 0) -> dict:
    key = jax.random.key(seed)
    ks = jax.random.split(key, 20)
    L = DEPTH
    nrm = lambda k, shape, s: jax.random.normal(k, shape, jnp.float32) * s
    return {
        "x": jax.random.normal(ks[0], (BATCH, SEQ, D_MODEL), jnp.float32),
        "norm1_g": 1.0 + nrm(ks[1], (L, D_MODEL), 0.01),
        "w_in": nrm(ks[2], (L, D_MODEL, D_IN), D_MODEL ** -0.5),
        "dw_w": nrm(ks[3], (L, CONV_WIDTH, 1, D_CONV), CONV_WIDTH ** -0.5),
        "dw_b": nrm(ks[4], (L, D_CONV), 0.01),
        "cln_g": 1.0 + nrm(ks[5], (L, D_CONV), 0.01),
        "cln_b": nrm(ks[6], (L, D_CONV), 0.01),
        "ck_pe": nrm(ks[7], (L, CMP_BLOCK, HEAD_DIM), 0.1),
        "ck_w1": nrm(ks[8], (L, CMP_BLOCK * HEAD_DIM, CMP_HIDDEN), (CMP_BLOCK * HEAD_DIM) ** -0.5),
        "ck_w2": nrm(ks[9], (L, CMP_HIDDEN, HEAD_DIM), CMP_HIDDEN ** -0.5),
        "cv_pe": nrm(ks[10], (L, CMP_BLOCK, HEAD_DIM), 0.1),
        "cv_w1": nrm(ks[11], (L, CMP_BLOCK * HEAD_DIM, CMP_HIDDEN), (CMP_BLOCK * HEAD_DIM) ** -0.5),
        "cv_w2": nrm(ks[12], (L, CMP_HIDDEN, HEAD_DIM), CMP_HIDDEN ** -0.5),
        "w_out": nrm(ks[13], (L, D_CONV + D_ATTN, D_MODEL), (D_CONV + D_ATTN) ** -0.5),
        "norm2_g": 1.0 + nrm(ks[14], (L, D_MODEL), 0.01),
        "w_ff1": nrm(ks[15], (L, D_MODEL, D_FF), D_MODEL ** -0.5),
        "w_ff2": nrm(ks[16], (L, D_FF, D_MODEL), D_FF ** -0.5),
        "norm_f_g": 1.0 + nrm(ks[17], (D_MODEL,), 0.01),
    }


def reference(x, norm1_g, w_in, dw_w, dw_b, cln_g, cln_b, ck_pe, ck_w1, ck_w2,
              cv_pe, cv_w1, cv_w2, w_out, norm2_g, w_ff1, w_ff2, norm_f_g):
    for l in range(DEPTH):
        x = hybrid_layer(x, norm1_g[l], w_in[l], dw_w[l], dw_b[l], cln_g[l], cln_b[l],
                         ck_pe[l], ck_w1[l], ck_w2[l], cv_pe[l], cv_w1[l], cv_w2[l],
                         w_out[l], norm2_g[l], w_ff1[l], w_ff2[l])
    return rmsnorm(x, norm_f_g)
```

```python
from contextlib import ExitStack
import numpy as np
import ml_dtypes
import concourse.bass as bass
import concourse.mybir as mybir
from concourse.bass_utils import run_bass_kernel_spmd

F32 = mybir.dt.float32
BF16 = mybir.dt.bfloat16
AF = mybir.ActivationFunctionType
ALU = mybir.AluOpType
AX = mybir.AxisListType

ENGS = ["tensor", "scalar", "vector", "gpsimd", "sync"]

D = 1024
DFF = 4096
DIN = 2328
EPS = 1e-6
BIGM = 32768.0
SLOPES = [2.0 ** (-(h + 1)) for h in range(8)]
USE_GELU_TANH = True


import types


def _freeze(fn):
    if fn.__closure__ is None:
        return fn
    cells = []
    for c in fn.__closure__:
        try:
            cells.append(types.CellType(c.cell_contents))
        except ValueError:
            cells.append(c)
    return types.FunctionType(fn.__code__, fn.__globals__, fn.__name__, fn.__defaults__, tuple(cells))

class Buf:
    __slots__ = ("name", "lw", "rd", "excl")

    def __init__(self, name, excl=False):
        self.name = name
        self.lw = None
        self.rd = []
        self.excl = excl


class Op:
    __slots__ = ("eng", "fn", "reads", "writes", "dma", "deps", "idx", "sig", "done", "waits")


class Tile:
    def __init__(self, h, name, psum=False):
        self.h = h
        self.name = name
        self.bufs = {}
        self.psum = psum

    def b(self, key=None):
        if self.psum:
            key = None
        if key not in self.bufs:
            self.bufs[key] = Buf("%s:%s" % (self.name, key), excl=self.psum)
        return self.bufs[key]

    def __getitem__(self, idx):
        return self.h[idx]


class Prog:
    def __init__(self, nc):
        self.nc = nc
        self.ops = []
        self.stack = ExitStack()
        self.barrier_at = []

    def sb(self, name, shape, dt):
        h = self.stack.enter_context(self.nc.sbuf_tensor("s_" + name, list(shape), dt))
        return Tile(h, name)

    def ps(self, name, shape, dt):
        h = self.stack.enter_context(self.nc.psum_tensor("p_" + name, list(shape), dt))
        return Tile(h, name, psum=True)

    def add(self, eng, fn, reads=(), writes=(), dma=None):
        o = Op()
        o.eng = eng
        o.fn = _freeze(fn)
        o.reads = [r for r in reads if r is not None and not r.excl]
        o.writes = [w for w in writes if w is not None] + [r for r in reads if r is not None and r.excl]
        o.dma = dma
        o.idx = len(self.ops)
        o.sig = dma is not None
        o.deps = set()
        o.waits = []
        self.ops.append(o)
        return o

    def barrier(self):
        self.barrier_at.append(len(self.ops))

    def analyze(self):
        ops = self.ops
        last_on_eng = {}
        barrier_set = set(self.barrier_at)
        pending_barrier = None
        seen_after_barrier = set()
        for o in ops:
            if o.idx in barrier_set:
                pending_barrier = dict(last_on_eng)
                seen_after_barrier = set()
            deps = set()
            for b in o.reads:
                if b.lw is not None:
                    deps.add(b.lw)
            for b in o.writes:
                if b.lw is not None:
                    deps.add(b.lw)
                for r in b.rd:
                    deps.add(r)
            if pending_barrier is not None and o.eng not in seen_after_barrier:
                seen_after_barrier.add(o.eng)
                for e, i in pending_barrier.items():
                    deps.add(i)
            for b in o.reads:
                b.rd.append(o.idx)
            for b in o.writes:
                b.lw = o.idx
                b.rd = []
            fin = set()
            for d in deps:
                if d == o.idx:
                    continue
                y = ops[d]
                if y.dma is None and y.eng == o.eng:
                    if o.eng in ("tensor", "sync"):
                        continue
                    yw = set(id(b) for b in y.writes)
                    touched = set(id(b) for b in o.reads) | set(id(b) for b in o.writes)
                    if not (yw & touched):
                        continue
                fin.add(d)
            o.deps = fin
            for d in fin:
                ops[d].sig = True
            last_on_eng[o.eng] = o.idx
            if o.dma is not None:
                last_on_eng['D_' + o.dma] = o.idx
        cnt = {}
        for o in ops:
            if o.dma is not None:
                k = "D_" + o.dma
                cnt[k] = cnt.get(k, 0) + 16
                o.done = (k, cnt[k])
            elif o.sig:
                k = "E_" + o.eng
                cnt[k] = cnt.get(k, 0) + 1
                o.done = (k, cnt[k])
            else:
                o.done = None
        self.final_counts = dict(cnt)
        known = {e: {} for e in ENGS}
        for o in ops:
            need = {}
            for d in o.deps:
                k, v = ops[d].done
                if v > need.get(k, 0):
                    need[k] = v
            kn = known[o.eng]
            o.waits = []
            for k, v in need.items():
                if kn.get(k, 0) >= v:
                    continue
                kn[k] = v
                o.waits.append((k, v))

    def emit(self, final_eng="sync"):
        self.analyze()
        nc = self.nc
        sems = {}
        for k in self.final_counts:
            sems[k] = self.stack.enter_context(nc.semaphore(k))
        self.nsem = len(sems)
        final_counts = self.final_counts
        with nc.Block() as block:
            for en in ENGS:
                eops = [o for o in self.ops if o.eng == en]

                def body(e, eops=eops, en=en):
                    for o in eops:
                        for (k, v) in o.waits:
                            e.wait_ge(sems[k], v)
                        ins = o.fn(e)
                        if o.sig:
                            ins.then_inc(sems[o.done[0]], 16 if o.dma is not None else 1)
                    if en == final_eng:
                        for k, v in final_counts.items():
                            if k.startswith("D_"):
                                e.wait_ge(sems[k], v)

                getattr(block, en)(body)


def make_consts():
    c = {}
    c["ident"] = np.eye(128, dtype=np.float32)
    p = np.arange(128)
    a = np.arange(512)
    a_hi, a_lo = a // 64, a % 64
    t0 = np.zeros((128, 512), np.float32)
    for j in range(64):
        t0[64 + j] = a_hi - j
    c["t0"] = t0
    dr = np.zeros((2, 128, 512), np.float32)
    for par in range(2):
        for jr in range(16):
            dr[par, 64 + jr] = (8 * par + a_hi - jr) % 16
    c["dr"] = dr
    oh16 = np.zeros((16, 1024), np.float32)
    for jr in range(16):
        oh16[jr, jr * 64:(jr + 1) * 64] = 1
    c["oh16"] = oh16
    ohc = np.zeros((64, 256), np.float32)
    for cp in range(256):
        ohc[cp // 4, cp] = 1
    c["ohc"] = ohc
    sl = np.array(SLOPES, np.float32)
    c["biask"] = ((p % 64)[:, None] * sl[None, :]).astype(np.float32)
    bc = np.zeros((128, 2, 8), np.float32)
    for ct in range(2):
        cp = ct * 128 + p
        bc[:, ct, :] = (16 * (cp % 4) + 15)[:, None] * sl[None, :]
    bc[0, 0, :] = -30000.0
    c["biasc"] = bc
    ova = np.zeros((128, 2, 65), np.float32)
    for ct in range(2):
        for pp in range(128):
            cp = ct * 128 + pp
            if cp == 0:
                continue
            cc = cp - 1
            for j in range(64):
                if (16 * cc < 64 * (j + 1)) and (16 * cc + 32 > 64 * j):
                    ova[pp, ct, j] = 1
    ova[:, :, 64] = 1
    c["ova"] = ova
    mc = np.ones((128, 4, 512), np.float32)
    for r in range(4):
        for pl in range(32):
            aa, m = pl // 4, pl % 4
            row = np.where(a_hi > aa, 1.0, np.where(a_hi == aa, (a_lo >= 16 * m + 15) * 1.0, 0.0))
            mc[32 * r + pl, r] = row
    c["mcf"] = mc
    a64 = np.arange(64)
    c["trid"] = (((p % 64)[:, None]) <= a64[None, :]).astype(np.float32)
    c["trif"] = (((p % 64)[:, None]) > a64[None, :]).astype(np.float32)
    addw = np.zeros((128, 128), np.float32)
    for pp in range(128):
        y = np.arange(128) - (pp >= 64)
        addw[pp] = np.where((y == 62) | (y == 63), 1e30, np.where(y > 63, -1e30, 0.0))
    c["addw"] = addw
    c["onesf"] = np.full((128, 128), 1.0 / 512, np.float32)
    return c


def layout_weights(inp):
    w = {}
    sq = lambda a: np.ascontiguousarray(np.asarray(a, np.float32).reshape(np.asarray(a).shape[1:]))
    w["g1"] = sq(inp["norm1_g"])
    w["g2"] = sq(inp["norm2_g"])
    w["gf"] = np.ascontiguousarray(np.asarray(inp["norm_f_g"], np.float32))
    w["w_in"] = sq(inp["w_in"])
    w["w_out"] = sq(inp["w_out"])
    w["w_ff1"] = sq(inp["w_ff1"])
    w["w_ff2"] = sq(inp["w_ff2"])
    dw = sq(inp["dw_w"]).reshape(31, 512)
    w["dww"] = np.ascontiguousarray(dw.T.reshape(4, 128, 31).transpose(1, 0, 2))
    for nm in ("dw_b", "cln_g", "cln_b"):
        w[nm] = np.ascontiguousarray(sq(inp[nm]).reshape(4, 128).T)
    w1c = np.zeros((2, 128, 16, 256), np.float32)
    pe2 = np.zeros((128, 2, 16), np.float32)
    w2c = np.zeros((2, 128, 2, 64), np.float32)
    for kv, (nw1, npe, nw2) in enumerate((("ck_w1", "ck_pe", "ck_w2"), ("cv_w1", "cv_pe", "cv_w2"))):
        a1 = sq(inp[nw1]).reshape(32, 64, 256)
        w1c[kv, 0:64] = a1[0:16].transpose(1, 0, 2)
        w1c[kv, 64:128] = a1[16:32].transpose(1, 0, 2)
        pe = sq(inp[npe])
        pe2[0:64, kv, :] = pe[0:16].T
        pe2[64:128, kv, :] = pe[16:32].T
        w2c[kv] = sq(inp[nw2]).reshape(2, 128, 64).transpose(1, 0, 2)
    w["w1c"] = w1c
    w["pe2"] = pe2
    w["w2c"] = w2c
    return w


class StopStage(Exception):
    pass


def build(T, debug=(), stop=99):
    NT = T // 512

    def stage(n):
        if n > stop:
            raise StopStage()
    nc = bass.Bass("TRN2", target_bir_lowering=False)
    P = Prog(nc)
    dram_in = {}

    def din(name, shape):
        dram_in[name] = nc.dram_tensor(name, list(shape), F32, kind="ExternalInput").ap()
        return dram_in[name]

    x = din("x", [T, D])
    g1 = din("g1", [D]); g2 = din("g2", [D]); gf = din("gf", [D])
    w_in = din("w_in", [D, DIN]); w_out = din("w_out", [D, D])
    w_ff1 = din("w_ff1", [D, DFF]); w_ff2 = din("w_ff2", [DFF, D])
    dww_d = din("dww", [128, 4, 31]); dwb_d = din("dw_b", [128, 4]); clg_d = din("cln_g", [128, 4]); clb_d = din("cln_b", [128, 4])
    w1c_d = din("w1c", [2, 128, 16, 256]); pe2_d = din("pe2", [128, 2, 16]); w2c_d = din("w2c", [2, 128, 2, 64])
    ident_d = din("ident", [128, 128]); t0_d = din("t0", [128, 512]); dr_d = din("dr", [2, 128, 512])
    oh16_d = din("oh16", [16, 1024]); ohc_d = din("ohc", [64, 256]); ohk_d = din("ohk", [64, T])
    biask_d = din("biask", [128, 8]); biasc_d = din("biasc", [128, 2, 8]); ova_d = din("ova", [128, 2, 65])
    mcf_d = din("mcf", [128, 4, 512]); trid_d = din("trid", [128, 64]); trif_d = din("trif", [128, 64])
    addw_d = din("addw", [128, 128]); onesf_d = din("onesf", [128, 128])
    out = nc.dram_tensor("out", [T, D], F32, kind="ExternalOutput").ap()
    x1d = nc.dram_tensor("x1d", [T, D], F32).ap()
    dbg_out = {}

    O_UV, O_UG, O_Q, O_KV, O_G = 0, 512, 1024, 1536, 2304
    kvcol = lambda i: O_KV + i * 128

    pT = P.ps("pT", [128, 8, 128], BF16)
    class PView:
        def __init__(self, ap, name):
            self.ap = ap
            self._b = Buf(name, excl=True)

        def b(self, key=None):
            return self._b

        def __getitem__(self, idx):
            return self.ap[idx]

    pAA = P.ps("pAA", [128, 1024], F32)
    pSS = P.ps("pSS", [128, 1024], F32)
    pA = [PView(pAA.h[:, i * 512:(i + 1) * 512], "pA%d" % i) for i in range(2)]
    pS = [PView(pSS.h[:, i * 512:(i + 1) * 512], "pS%d" % i) for i in range(2)]
    pPair = [(pSS.h, pS)]
    pO = [P.ps("pO%d" % i, [128, 512], F32) for i in range(2)]
    pX = P.ps("pX", [128, 512], F32)

    cst = ExitStack()
    identb = P.sb("identb", [128, 128], BF16)
    identf = P.sb("identf", [128, 128], F32)
    gbt = P.sb("gbt", [128, D], F32)
    xt = P.sb("xt", [128, 4, D], F32)
    ssq = P.sb("ssq", [128, 4], F32)
    rstd = P.sb("rstd", [128, 4], F32)
    htok = [P.sb("htok%d" % i, [128, D], BF16) for i in range(2)]
    junk = htok[1]
    hT = P.sb("hT", [128, 8, 512], BF16)

    def ld(eng, dst_ap, src_ap, wbufs, key):
        P.add(eng, lambda e: e.dma_start(out=dst_ap, in_=src_ap), writes=wbufs, dma=key)

    ld("gpsimd", identb[:], ident_d, [identb.b()], "c_identb")
    ld("sync", identf[:], ident_d, [identf.b()], "c_identf")

    def norm_stats(xc):
        for s in range(4):
            P.add("scalar", lambda e, s=s: e.activation(out=junk[:], in_=xc[:, s, :], func=AF.Square,
                                                        accum_out=ssq[:, s:s + 1]),
                  reads=[xc.b(s)], writes=[junk.b(), ssq.b(s)])
            P.add("scalar", lambda e, s=s: e.activation(out=rstd[:, s:s + 1], in_=ssq[:, s:s + 1], func=AF.Sqrt,
                                                        scale=1.0 / D, bias=EPS),
                  reads=[ssq.b(s)], writes=[rstd.b(s)])
            P.add("vector", lambda e, s=s: e.reciprocal(out=rstd[:, s:s + 1], in_=rstd[:, s:s + 1]),
                  reads=[rstd.b(s)], writes=[rstd.b(s)])

    def norm_apply(xc, gtile):
        for s in range(4):
            ht = htok[s % 2]
            P.add("vector", lambda e, s=s, ht=ht: e.scalar_tensor_tensor(
                out=ht[:], in0=xc[:, s, :], scalar=rstd[:, s:s + 1], in1=gtile[:], op0=ALU.mult, op1=ALU.mult),
                reads=[xc.b(s), rstd.b(s), gtile.b()], writes=[ht.b()])
            for kc in range(8):
                P.add("tensor", lambda e, kc=kc, ht=ht: e.transpose(out=pT[:, kc, :], in_=ht[:, kc * 128:(kc + 1) * 128],
                                                                   identity=identb[:]),
                      reads=[ht.b(), identb.b()], writes=[pT.b()])
            P.add("vector" if s % 2 else "scalar", lambda e, s=s: (e.tensor_copy if s % 2 else e.copy)(
                out=hT[:, :, s * 128:(s + 1) * 128], in_=pT[:]),
                reads=[pT.b()], writes=[hT.b(s)])

    def load_x(src_v, qt_, s, key, xc=None):
        xc = xc if xc is not None else xt
        P.add("sync", lambda e: e.dma_start(out=xc[:, s, :], in_=src_v[qt_][:, s, :]), writes=[xc.b(s)],
              dma="%s%d" % (key, s))

    hTall = [hT.b(s) for s in range(4)]

    def dbg(name, src_ap, shape, rbufs):
        if name not in debug:
            return
        d = nc.dram_tensor("dbg_" + name, list(shape), src_ap.dtype, kind="ExternalOutput").ap()
        dbg_out[name] = d
        P.add("sync", lambda e: e.dma_start(out=d, in_=src_ap), reads=rbufs, dma="dbg_" + name)

    A = ExitStack()
    P.stack, outer_stack = A, P.stack
    wv = P.sb("wv", [128, 8, 280], BF16)
    wout = P.sb("wout", [128, 8, D], BF16)
    w1c = P.sb("w1c", [128, 2, 16, 256], BF16)
    w2c = P.sb("w2c", [128, 2, 2, 64], BF16)
    pe2 = P.sb("pe2", [128, 2, 16], BF16)
    b1 = P.sb("b1", [128, 2, 2], F32)
    dww = P.sb("dww", [128, 4, 31], F32)
    dwb = P.sb("dwb", [128, 4], F32); clg = P.sb("clg", [128, 4], F32); clb = P.sb("clb", [128, 4], F32)
    t0 = P.sb("t0", [128, 512], BF16)
    drt = P.sb("drt", [128, 2, 512], BF16)
    biask = P.sb("biask", [128, 8], F32); biasc = P.sb("biasc", [128, 2, 8], F32)
    ova = P.sb("ova", [128, 2, 65], BF16)
    mcf = P.sb("mcf", [128, 4, 512], BF16); trid = P.sb("trid", [128, 64], BF16); trif = P.sb("trif", [128, 64], BF16)
    addw = P.sb("addw", [128, 128], F32)
    onesf = P.sb("onesf", [128, 128], F32)
    Ks = P.sb("Ks", [128, 2, T], BF16)
    Kw = P.sb("Kw", [128, 2, 1024], BF16)
    Vs = P.sb("Vs", [128, T // 128, 2, 65], BF16)
    Vw = P.sb("Vw", [128, 8, 2, 65], BF16)
    Kc = P.sb("Kc", [128, 2, 256], BF16)
    vcT = P.sb("vcT", [64, 2, 256], BF16)
    vcs = P.sb("vcs", [128, 2, 2, 64], BF16)
    wb = [P.sb("wb%d" % i, [128, 8, 128], BF16) for i in range(3)]
    xs = P.sb("xs", [128, D], F32)
    dgbA = P.sb("dgbA", [128, 16, 128], BF16)
    dgbB = P.sb("dgbB", [128, 15, 128], BF16)
    Gt = P.sb("Gt", [128, 4, 24], F32)
    ub = [P.sb("ub%d" % i, [128, 4, 542], BF16) for i in range(1)]
    sg = P.sb("sg", [128, 512], BF16)
    yb = P.sb("yb", [128, 4, 512], F32)
    lnA = P.sb("lnA", [128, 512], F32); lnB = P.sb("lnB", [128, 512], F32); lnC = P.sb("lnC", [128, 512], F32)
    AT = P.sb("AT", [128, 8, 512], BF16)
    KK = [P.sb("KK%d" % i, [128, 4, 544], BF16) for i in range(1)]
    hid = P.sb("hid", [128, 2, 2, 64], BF16)
    hx = P.sb("hx", [128, 64], F32); hy = P.sb("hy", [128, 64], F32)
    Qc = P.sb("Qc", [128, 4, 512], BF16); Qs = P.sb("Qs", [128, 4, 512], BF16); Qw = P.sb("Qw", [128, 4, 512], BF16)
    Dq = P.sb("Dq", [128, 512], BF16); NEGM = P.sb("NEGM", [128, 512], BF16); WBR = P.sb("WBR", [128, 512], BF16)
    Pb2 = [P.sb("Pb2_%d" % i, [128, 1024], BF16) for i in range(4)]

    class SView:
        def __init__(self, ap, buf):
            self.ap = ap
            self._b = buf

        def b(self, key=None):
            return self._b

        def __getitem__(self, idx):
            return self.ap[idx]

    Pb = [SView(Pb2[i // 2].h[:, (i % 2) * 512:(i % 2 + 1) * 512], Pb2[i // 2].b(i % 2)) for i in range(8)]
    OT = [P.sb("OT%d" % i, [65, 512], F32) for i in range(2)]
    acc4 = [P.sb("acc%d" % i, [128, 4, 64], F32) for i in range(4)]
    tmpc = P.sb("tmpc", [128, 4, 64], F32)
    atok = P.sb("atok", [128, 4, 512], BF16)
    impacc = P.sb("impacc", [128, 4, 64], F32)
    impm = P.sb("impm", [128, 64], F32); imp2 = P.sb("imp2", [128, 64], F32)
    m8a = P.sb("m8a", [128, 8], F32); m8b = P.sb("m8b", [128, 8], F32)
    selb = P.sb("selb", [128, 4, 128], BF16)
    rden = P.sb("rden", [128, 4], F32); scg = P.sb("scg", [128, 4], F32)

    cstq = ["sync", "gpsimd"]
    ld("sync", gbt[:], g1.partition_broadcast(128), [gbt.b()], "c_gbt")
    for kc in range(8):
        wiv = w_in[kc * 128:(kc + 1) * 128, :]
        ld("gpsimd", wv[:, kc, 0:128], wiv[:, kvcol(3):kvcol(3) + 128], [wv.b()], "c_wv")
        ld("gpsimd", wv[:, kc, 128:256], wiv[:, kvcol(5):kvcol(5) + 128], [wv.b()], "c_wv")
        ld("gpsimd", wv[:, kc, 256:280], wiv[:, O_G:O_G + 24], [wv.b()], "c_wv")
    ld("gpsimd", wout[:], w_out.rearrange("(kc p) n -> p kc n", p=128), [wout.b()], "c_wout")
    for kv in range(2):
        ld("gpsimd", w1c[:, kv], w1c_d[kv], [w1c.b()], "c_w1c")
        ld("gpsimd", w2c[:, kv], w2c_d[kv], [w2c.b()], "c_w2c")
    ld("gpsimd", pe2[:], pe2_d, [pe2.b()], "c_pe2")
    for (tl, dd, k) in ((dww, dww_d, "dww"), (dwb, dwb_d, "dwb"), (clg, clg_d, "clg"), (clb, clb_d, "clb"),
                        (biask, biask_d, "biask"), (biasc, biasc_d, "biasc"), (addw, addw_d, "addw"),
                        (onesf, onesf_d, "onesf")):
        ld("sync", tl[:], dd, [tl.b()], "c_" + k)
    for (tl, dd, k) in ((t0, t0_d, "t0"), (ova, ova_d, "ova"), (mcf, mcf_d, "mcf"), (trid, trid_d, "trid"),
                        (trif, trif_d, "trif")):
        ld("gpsimd", tl[:], dd, [tl.b()], "c_" + k)
    for par in range(2):
        ld("gpsimd", drt[:, par, :], dr_d[par], [drt.b()], "c_drt")
    P.add("vector", lambda e: e.memset(Ks[:], 0.0), writes=[Ks.b("z")])
    P.add("vector", lambda e: e.memset(Kw[:], 0.0), writes=[Kw.b("z")])
    P.add("vector", lambda e: e.memset(Kc[:], 0.0), writes=[Kc.b("z")])
    P.add("vector", lambda e: e.memset(vcT[:], 0.0), writes=[vcT.b()])
    P.add("vector", lambda e: e.memset(vcs[:], 0.0), writes=[vcs.b()])
    P.add("vector", lambda e: e.memset(Vs[:], 1.0), writes=[Vs.b("z")])
    P.add("vector", lambda e: e.memset(Vw[:], 1.0), writes=[Vw.b("z")])
    P.add("vector", lambda e: e.memset(Qw[:], 0.0), writes=[Qw.b("z")])
    P.add("vector", lambda e: e.memset(selb[:], 0.0), writes=[selb.b(s) for s in range(4)])
    for i in range(1):
        P.add("vector", lambda e, i=i: e.memset(KK[i][:], 0.0), writes=[KK[i].b()])
        P.add("vector", lambda e, i=i: e.memset(ub[i][:], 0.0), writes=[ub[i].b(c) for c in range(4)])
    for g in range(2):
        P.add("gpsimd", lambda e, g=g: e.dma_start(out=Ks[64:128, g, :], in_=ohk_d), reads=[Ks.b("z")],
              writes=[Ks.b("oh")], dma="c_ohk")
        P.add("gpsimd", lambda e, g=g: e.dma_start(out=Kw[64:80, g, :], in_=oh16_d), reads=[Kw.b("z")],
              writes=[Kw.b("oh")], dma="c_oh16")
        P.add("gpsimd", lambda e, g=g: e.dma_start(out=Kc[64:128, g, :], in_=ohc_d), reads=[Kc.b("z")],
              writes=[Kc.b("oh")], dma="c_ohc")
    for kv in range(2):
        for mc in range(2):
            for l in range(16):
                P.add("tensor", lambda e, kv=kv, mc=mc, l=l: e.matmul(
                    pX[:, kv * 2 + mc:kv * 2 + mc + 1], lhsT=w1c[:, kv, l, mc * 128:(mc + 1) * 128],
                    rhs=pe2[:, kv, l:l + 1], start=(l == 0), stop=(l == 15)),
                    reads=[w1c.b(), pe2.b()], writes=[pX.b()])
    P.add("vector", lambda e: e.tensor_copy(out=b1[:], in_=pX[:, 0:4].rearrange("p (a b) -> p a b", a=2)),
          reads=[pX.b()], writes=[b1.b()])

    wreq = []
    for qt in range(NT):
        wreq.append([(0, 128, kvcol(2))])
        wreq.append([(0, 128, kvcol(4))])
        for i in (0, 1):
            for g in range(2):
                wreq.append([(0, 64, kvcol(i) + g * 64), (64, 64, kvcol(i) + g * 64)])
        for hp in range(2):
            wreq.append([(0, 128, O_Q + hp * 128)])
        for c4 in range(4):
            wreq.append([(0, 128, O_UG + c4 * 128)])
            wreq.append([(0, 128, O_UV + c4 * 128)])
        for hp in range(2, 4):
            wreq.append([(0, 128, O_Q + hp * 128)])
    wstate = {"issued": 0}
    w_in_v = w_in.rearrange("(kc p) n -> p kc n", p=128)

    NGRP = len(wreq) // NT
    wsc = nc.dram_tensor("wsc", [NGRP, 128, 1024], BF16).ap()
    wsc_b = [Buf("wsc%d" % i) for i in range(NGRP)]

    def w_ensure(upto):
        while wstate["issued"] <= min(upto, len(wreq) - 1):
            i = wstate["issued"]
            t = wb[i % 3]
            if i < NGRP:
                for (c0, ncol, s0) in wreq[i]:
                    P.add("gpsimd", lambda e, t=t, c0=c0, ncol=ncol, s0=s0: e.dma_start(
                        out=t[:, :, c0:c0 + ncol], in_=w_in_v[:, :, s0:s0 + ncol]),
                        writes=[t.b()], dma="wb%d" % (i % 3))
                P.add("sync", lambda e, t=t, i=i: e.dma_start(out=wsc[i], in_=t[:].rearrange("p a b -> p (a b)")),
                      reads=[t.b()], writes=[wsc_b[i]], dma="wsc%d" % (i % 3))
            else:
                gi = i % NGRP
                P.add("sync", lambda e, t=t, gi=gi: e.dma_start(out=t[:].rearrange("p a b -> p (a b)"), in_=wsc[gi]),
                      reads=[wsc_b[gi]], writes=[t.b()], dma="wb%d" % (i % 3))
            wstate["issued"] += 1

    wctr = {"i": 0}

    def next_w():
        i = wctr["i"]
        w_ensure(i + 1)
        wctr["i"] += 1
        return wb[i % 3]

    pa_ctr = {"i": 0}

    def fm_group(wt, c0, M):
        pa = pA[pa_ctr["i"] % 2]
        pa_ctr["i"] += 1
        for kc in range(8):
            P.add("tensor", lambda e, kc=kc, pa=pa: e.matmul(pa[0:M, :], lhsT=wt[:, kc, c0:c0 + M], rhs=hT[:, kc, :],
                                                            start=(kc == 0), stop=(kc == 7)),
                  reads=hTall + [wt.b()], writes=[pa.b()])
        return pa

    dg_ctr = {"i": 0}
    ps_ctr = {"i": 0}
    pb_ctr = {"i": 0}

    xv = x.rearrange("(q s p) d -> q p s d", p=128, s=4)
    x1v = x1d.rearrange("(q s p) d -> q p s d", p=128, s=4)
    outv = out.rearrange("(q s p) d -> q p s d", p=128, s=4)

    for qt in range(NT):
      try:
        par = qt % 2
        q0 = qt * 512
        ctm = qt // 4
        if qt == 0:
            for s_ in range(4):
                load_x(xv, 0, s_, "xt")
            norm_stats(xt)
        norm_apply(xt, gbt)
        if qt + 1 < NT:
            for s_ in range(4):
                load_x(xv, qt + 1, s_, "xt")
        stage(1)
        for s in range(4):
            kt = 4 * qt + s
            for kc in range(8):
                P.add("tensor", lambda e, kc=kc, s=s: e.matmul(pX[:, 0:280], lhsT=hT[:, kc, s * 128:(s + 1) * 128],
                                                               rhs=wv[:, kc, :], start=(kc == 0), stop=(kc == 7)),
                      reads=[hT.b(s), wv.b()], writes=[pX.b()])
            P.add("scalar", lambda e, kt=kt: e.copy(out=Vs[:, kt, :, 0:64],
                                                    in_=pX[:, 0:128].rearrange("p (g d) -> p g d", g=2)),
                  reads=[pX.b(), Vs.b("z")], writes=[Vs.b(kt)])
            P.add("vector", lambda e, kt=kt: e.tensor_copy(out=Vw[:, kt % 8, :, 0:64],
                                                           in_=pX[:, 128:256].rearrange("p (g d) -> p g d", g=2)),
                  reads=[pX.b(), Vw.b("z")], writes=[Vw.b(kt % 8)])
            P.add("scalar", lambda e, s=s: e.activation(out=Gt[:, s, :], in_=pX[:, 256:280], func=AF.Sigmoid),
                  reads=[pX.b()], writes=[Gt.b(s)])
        stage(2)
        wt = next_w()
        for g in range(2):
            pa = fm_group(wt, g * 64, 64)
            P.add("scalar", lambda e, g=g, pa=pa: e.copy(out=Ks[0:64, g, q0:q0 + 512], in_=pa[0:64, :]),
                  reads=[pa.b(), Ks.b("z")], writes=[Ks.b((g, qt))])
        wt = next_w()
        r0 = (qt % 2) * 512
        for g in range(2):
            pa = fm_group(wt, g * 64, 64)
            P.add("scalar", lambda e, g=g, pa=pa: e.copy(out=Kw[0:64, g, r0:r0 + 512], in_=pa[0:64, :]),
                  reads=[pa.b(), Kw.b("z")], writes=[Kw.b((g, qt % 2))])
        stage(3)
        kk = KK[0]
        kko = KK[0]
        if qt > 0:
            P.add("gpsimd", lambda e, kk=kk, kko=kko: e.tensor_copy(out=kk[0:64, :, 0:32], in_=kko[0:64, :, 512:544]),
                  reads=[kko.b()], writes=[kk.b()])
            P.add("gpsimd", lambda e, kk=kk, kko=kko: e.tensor_copy(out=kk[64:128, :, 0:16], in_=kko[64:128, :, 512:528]),
                  reads=[kko.b()], writes=[kk.b()])
        for i in (0, 1):
            for g in range(2):
                wt = next_w()
                pa = fm_group(wt, 0, 128)
                P.add("scalar", lambda e, i=i, g=g, pa=pa, kk=kk: e.copy(out=kk[0:64, 2 * i + g, 32:544], in_=pa[0:64, :]),
                      reads=[pa.b()], writes=[kk.b()])
                P.add("vector", lambda e, i=i, g=g, pa=pa, kk=kk: e.tensor_copy(out=kk[64:128, 2 * i + g, 16:528],
                                                                               in_=pa[64:128, :]),
                      reads=[pa.b()], writes=[kk.b()])
        stage(4)
        for kv in range(2):
            for mc in range(2):
                for l in range(16):
                    P.add("tensor", lambda e, kv=kv, mc=mc, l=l, kk=kk: e.matmul(
                        pX[:, 0:64], lhsT=w1c[:, kv, l, mc * 128:(mc + 1) * 128],
                        rhs=kk[:, 2 * kv:2 * kv + 2, 16 + l:16 + l + 512:16], start=(l == 0), stop=(l == 15)),
                        reads=[w1c.b(), kk.b()], writes=[pX.b()])
                if USE_GELU_TANH:
                    P.add("scalar", lambda e, kv=kv, mc=mc: e.activation(
                        out=hid[:, kv, mc, :], in_=pX[:, 0:64], func=AF.Gelu_apprx_tanh, bias=b1[:, kv, mc:mc + 1]),
                        reads=[pX.b(), b1.b()], writes=[hid.b()])
                else:
                    P.add("scalar", lambda e, kv=kv, mc=mc: e.activation(
                        out=hx[:], in_=pX[:, 0:64], func=AF.Identity, bias=b1[:, kv, mc:mc + 1]),
                        reads=[pX.b(), b1.b()], writes=[hx.b()])
                    P.add("vector", lambda e: e.tensor_tensor(out=hy[:], in0=hx[:], in1=hx[:], op=ALU.mult),
                          reads=[hx.b()], writes=[hy.b()])
                    P.add("vector", lambda e: e.tensor_scalar(out=hy[:], in0=hy[:], scalar1=0.044715, scalar2=1.0,
                                                              op0=ALU.mult, op1=ALU.add),
                          reads=[hy.b()], writes=[hy.b()])
                    P.add("vector", lambda e: e.tensor_tensor(out=hy[:], in0=hy[:], in1=hx[:], op=ALU.mult),
                          reads=[hy.b(), hx.b()], writes=[hy.b()])
                    P.add("scalar", lambda e: e.activation(out=hy[:], in_=hy[:], func=AF.Sigmoid, scale=1.5957691216),
                          reads=[hy.b()], writes=[hy.b()])
                    P.add("vector", lambda e, kv=kv, mc=mc: e.tensor_tensor(out=hid[:, kv, mc, :], in0=hy[:], in1=hx[:],
                                                                           op=ALU.mult),
                          reads=[hy.b(), hx.b()], writes=[hid.b()])
        for kv in range(2):
            for mc in range(2):
                P.add("tensor", lambda e, kv=kv, mc=mc: e.matmul(pX[0:64, kv * 64:(kv + 1) * 64], lhsT=w2c[:, kv, mc, :],
                                                                 rhs=hid[:, kv, mc, :], start=(mc == 0), stop=(mc == 1)),
                      reads=[w2c.b(), hid.b()], writes=[pX.b()])
        c0 = 32 * qt
        P.add("scalar", lambda e, c0=c0: e.copy(out=Kc[0:64, :, c0:c0 + 32],
                                                in_=pX[0:64, 0:64].rearrange("p (g n) -> p g n", g=2)),
              reads=[pX.b(), Kc.b("z")], writes=[Kc.b("k")])
        P.add("vector", lambda e, c0=c0: e.tensor_copy(out=vcT[:, :, c0:c0 + 32],
                                                       in_=pX[0:64, 64:128].rearrange("p (g n) -> p g n", g=2)),
              reads=[pX.b()], writes=[vcT.b()])
        for g in range(2):
            P.add("tensor", lambda e, g=g: e.transpose(out=pT[:, g, 0:64], in_=vcT[:, g, ctm * 128:(ctm + 1) * 128],
                                                       identity=identb[0:64, 0:64]),
                  reads=[vcT.b(), identb.b()], writes=[pT.b()])
        P.add("vector", lambda e: e.tensor_copy(out=vcs[:, ctm, :, :], in_=pT[:, 0:2, 0:64]),
              reads=[pT.b()], writes=[vcs.b()])
        stage(5)
        u = ub[0]
        uo = ub[0]
        if qt > 0:
            P.add("gpsimd", lambda e, u=u, uo=uo: e.tensor_copy(out=u[:, :, 0:30], in_=uo[:, :, 512:542]),
                  reads=[uo.b(c) for c in range(4)], writes=[u.b(c) for c in range(4)])
        conv_pa = {}

        def conv_part(c4, part):
            if part == 0:
                conv_pa[c4] = pA[pa_ctr["i"] % 2]
                pa_ctr["i"] += 1
                for hf, (dgt, k0, nk) in enumerate(((dgbA, 0, 16), (dgbB, 16, 15))):
                    P.add("vector", lambda e, dgt=dgt, k0=k0, nk=nk: e.tensor_tensor(
                        out=dgt[:], in0=identb[:].unsqueeze(1).to_broadcast([128, nk, 128]),
                        in1=dww[:, c4, k0:k0 + nk].unsqueeze(2).to_broadcast([128, nk, 128]), op=ALU.mult),
                        reads=[identb.b(), dww.b()], writes=[dgt.b()])
            pa = conv_pa[c4]
            for k in range(8 * part, min(31, 8 * part + 8)):
                dgt, kk_ = (dgbA, k) if k < 16 else (dgbB, k - 16)
                P.add("tensor", lambda e, k=k, dgt=dgt, kk_=kk_: e.matmul(pa[:], lhsT=dgt[:, kk_, :], rhs=u[:, c4, k:k + 512],
                                                                        start=(k == 0), stop=(k == 30)),
                      reads=[dgt.b(), u.b(c4)], writes=[pa.b()])
            if part == 3:
                P.add("scalar", lambda e: e.activation(out=yb[:, c4, :], in_=pa[:], func=AF.Identity,
                                                       bias=dwb[:, c4:c4 + 1]),
                      reads=[pa.b(), dwb.b()], writes=[yb.b(c4)])

        def u_gate(c4):
            wt = next_w()
            pg = fm_group(wt, 0, 128)
            P.add("scalar", lambda e, pg=pg: e.activation(out=sg[:], in_=pg[:], func=AF.Sigmoid),
                  reads=[pg.b()], writes=[sg.b()])

        def u_val(c4):
            wt = next_w()
            pv = fm_group(wt, 0, 128)
            P.add("vector", lambda e, pv=pv, c4=c4, u=u: e.tensor_tensor(out=u[:, c4, 30:542], in0=pv[:], in1=sg[:],
                                                                        op=ALU.mult),
                  reads=[pv.b(), sg.b()], writes=[u.b(c4)])

        P.add("gpsimd", lambda e: e.tensor_scalar(out=Dq[64:128, :], in0=t0[64:128, :], scalar1=float(8 * qt),
                                                  scalar2=None, op0=ALU.add),
              reads=[t0.b()], writes=[Dq.b()])
        P.add("gpsimd", lambda e: e.tensor_scalar(out=NEGM[64:128, :], in0=Dq[64:128, :], scalar1=0.0, scalar2=-BIGM,
                                                  op0=ALU.is_lt, op1=ALU.mult),
              reads=[Dq.b()], writes=[NEGM.b()])
        P.add("gpsimd", lambda e, par=par: e.tensor_scalar(out=WBR[64:80, :], in0=drt[64:80, par, :], scalar1=8.0,
                                                           scalar2=-BIGM, op0=ALU.is_gt, op1=ALU.mult),
              reads=[drt.b()], writes=[WBR.b()])
        def q_proj(g):
            for hp in range(2):
                wt = next_w()
                for hh in range(2):
                    hl = hp * 2 + hh
                    h = 4 * g + hl
                    pa = fm_group(wt, hh * 64, 64)
                    P.add("scalar", lambda e, pa=pa, hl=hl: e.mul(out=Qc[0:64, hl, :], in_=pa[0:64, :], mul=0.125),
                          reads=[pa.b()], writes=[Qc.b(hl)])
                    P.add("gpsimd", lambda e, hl=hl: e.tensor_copy(out=Qw[0:64, hl, :], in_=Qc[0:64, hl, :]),
                          reads=[Qc.b(hl), Qw.b("z")], writes=[Qw.b(hl)])
                    P.add("gpsimd", lambda e, hl=hl: e.tensor_copy(out=Qs[0:64, hl, :], in_=Qc[0:64, hl, :]),
                          reads=[Qc.b(hl)], writes=[Qs.b(hl)])
                    P.add("vector", lambda e, hl=hl, h=h: e.scalar_tensor_tensor(
                        out=Qc[64:128, hl, :], in0=Dq[64:128, :], scalar=-64.0 * SLOPES[h], in1=NEGM[64:128, :],
                        op0=ALU.mult, op1=ALU.add), reads=[Dq.b(), NEGM.b()], writes=[Qc.b((hl, "b"))])
                    P.add("vector", lambda e, hl=hl, h=h, par=par: e.scalar_tensor_tensor(
                        out=Qw[64:80, hl, :], in0=drt[64:80, par, :], scalar=-64.0 * SLOPES[h], in1=WBR[64:80, :],
                        op0=ALU.mult, op1=ALU.add), reads=[drt.b(), WBR.b(), Qw.b("z")], writes=[Qw.b((hl, "b"))])
        def make_cmp(g):
            cstate = {}

            def cmp_front(hl):
                h = 4 * g + hl
                pcs = []
                for ct in range(ctm + 1):
                    psx = pS[ps_ctr["i"] % 2]
                    ps_ctr["i"] += 1
                    P.add("tensor", lambda e, ct=ct, psx=psx: e.matmul(
                        psx[:], lhsT=Kc[:, g, ct * 128:(ct + 1) * 128], rhs=Qc[:, hl, :], start=True, stop=True),
                        reads=[Kc.b("k"), Kc.b("oh"), Qc.b(hl), Qc.b((hl, "b"))], writes=[psx.b()])
                    pb = Pb[pb_ctr["i"] % 8]
                    pb_ctr["i"] += 1
                    P.add("scalar", lambda e, ct=ct, psx=psx, pb=pb: e.activation(
                        out=pb[:], in_=psx[:], func=AF.Exp, bias=biasc[:, ct, h:h + 1]),
                        reads=[psx.b(), biasc.b()], writes=[pb.b()])
                    if ct == ctm:
                        P.add("vector", lambda e, pb=pb: e.tensor_tensor(out=pb[:], in0=pb[:], in1=mcf[:, qt % 4, :],
                                                                        op=ALU.mult),
                              reads=[pb.b(), mcf.b()], writes=[pb.b()])
                    pcs.append(pb)
                cstate[hl] = pcs

            def cmp_back(hl):
                pcs = cstate.pop(hl)
                tA = (pX, pA[0])[hl % 2]
                tB = (pO[1], pO[0])[hl % 2]
                psA = tA[:, 0:260].rearrange("p (s d) -> p s d", s=4)
                psB = tB[:, 0:256].rearrange("p (s d) -> p s d", s=4)
                for s in range(4):
                    for ct in range(ctm + 1):
                        pb = pcs[ct]
                        P.add("tensor", lambda e, s=s, ct=ct, pb=pb: e.matmul(
                            psA[:, s, :], lhsT=pb[:, s * 128:(s + 1) * 128], rhs=ova[:, ct, :], start=(ct == 0),
                            stop=(ct == ctm)), reads=[pb.b(), ova.b()], writes=[tA.b()])
                        P.add("tensor", lambda e, s=s, ct=ct, pb=pb: e.matmul(
                            psB[:, s, :], lhsT=pb[:, s * 128:(s + 1) * 128], rhs=vcs[:, ct, g, :], start=(ct == 0),
                            stop=(ct == ctm)), reads=[pb.b(), vcs.b()], writes=[tB.b()])
                P.add("vector", lambda e: e.tensor_scalar(out=rden[:], in0=psA[:, :, 64], scalar1=1e-30,
                                                          scalar2=None, op0=ALU.max),
                      reads=[tA.b()], writes=[rden.b()])
                P.add("vector", lambda e: e.reciprocal(out=rden[:], in_=rden[:]), reads=[rden.b()], writes=[rden.b()])
                if hl == 0:
                    P.add("vector", lambda e: e.tensor_tensor(
                        out=impacc[:], in0=psA[:, :, 0:64], in1=rden[:, :].unsqueeze(2).to_broadcast([128, 4, 64]),
                        op=ALU.mult), reads=[tA.b(), rden.b()], writes=[impacc.b()])
                else:
                    P.add("vector", lambda e: e.tensor_tensor(
                        out=tmpc[:], in0=psA[:, :, 0:64], in1=rden[:, :].unsqueeze(2).to_broadcast([128, 4, 64]),
                        op=ALU.mult), reads=[tA.b(), rden.b()], writes=[tmpc.b()])
                    P.add("vector", lambda e: e.tensor_tensor(out=impacc[:], in0=impacc[:], in1=tmpc[:], op=ALU.add),
                          reads=[impacc.b(), tmpc.b()], writes=[impacc.b()])
                gi = (g * 4 + hl) * 3
                P.add("vector", lambda e: e.tensor_tensor(out=scg[:], in0=rden[:], in1=Gt[:, :, gi], op=ALU.mult),
                      reads=[rden.b()] + [Gt.b(s) for s in range(4)], writes=[scg.b()])
                P.add("vector", lambda e: e.tensor_tensor(
                    out=acc4[hl][:], in0=psB[:, :, :], in1=scg[:, :].unsqueeze(2).to_broadcast([128, 4, 64]),
                    op=ALU.mult), reads=[tB.b(), scg.b()], writes=[acc4[hl].b()])

            return cmp_front, cmp_back

        q_proj(0)
        cf0, cb0 = make_cmp(0)
        cf0(0)
        for hl_ in range(4):
            if hl_ + 1 < 4:
                cf0(hl_ + 1)
            cb0(hl_)
        stage(6)
        filler = []
        for c4_ in range(4):
            filler.append(lambda c4_=c4_: u_gate(c4_))
            filler.append(lambda c4_=c4_: u_val(c4_))
            if c4_ >= 1:
                for part_ in range(4):
                    filler.append(lambda c4_=c4_, part_=part_: conv_part(c4_ - 1, part_))
        for part_ in range(4):
            filler.append(lambda part_=part_: conv_part(3, part_))

        def _ln_mean():
            for c4 in range(4):
                P.add("tensor", lambda e, c4=c4: e.matmul(pA[0][:], lhsT=onesf[:], rhs=yb[:, c4, :], start=(c4 == 0),
                                                          stop=(c4 == 3)),
                      reads=[onesf.b(), yb.b(c4)], writes=[pA[0].b()])

        def _ln_sq():
            for c4 in range(4):
                P.add("scalar", lambda e, c4=c4: e.activation(out=lnA[:], in_=yb[:, c4, :], func=AF.Square),
                      reads=[yb.b(c4)], writes=[lnA.b()])
                P.add("tensor", lambda e, c4=c4: e.matmul(pA[1][:], lhsT=onesf[:], rhs=lnA[:], start=(c4 == 0),
                                                          stop=(c4 == 3)),
                      reads=[onesf.b(), lnA.b()], writes=[pA[1].b()])

        def _ln_var():
            P.add("scalar", lambda e: e.copy(out=lnB[:], in_=pA[0][:]), reads=[pA[0].b()], writes=[lnB.b()])
            P.add("scalar", lambda e: e.activation(out=lnA[:], in_=pA[0][:], func=AF.Square), reads=[pA[0].b()],
                  writes=[lnA.b()])
            P.add("vector", lambda e: e.tensor_tensor(out=lnC[:], in0=pA[1][:], in1=lnA[:], op=ALU.subtract),
                  reads=[pA[1].b(), lnA.b()], writes=[lnC.b()])
            P.add("scalar", lambda e: e.activation(out=lnC[:], in_=lnC[:], func=AF.Sqrt, bias=EPS), reads=[lnC.b()],
                  writes=[lnC.b()])
            P.add("vector", lambda e: e.reciprocal(out=lnC[:], in_=lnC[:]), reads=[lnC.b()], writes=[lnC.b()])

        def _ln_chunk(c4):
            P.add("vector", lambda e: e.tensor_tensor(out=lnA[:], in0=yb[:, c4, :], in1=lnB[:], op=ALU.subtract),
                  reads=[yb.b(c4), lnB.b()], writes=[lnA.b()])
            P.add("vector", lambda e: e.tensor_tensor(out=lnA[:], in0=lnA[:], in1=lnC[:], op=ALU.mult),
                  reads=[lnA.b(), lnC.b()], writes=[lnA.b()])
            P.add("scalar", lambda e: e.activation(out=AT[:, c4, :], in_=lnA[:], func=AF.Silu,
                                                   scale=clg[:, c4:c4 + 1], bias=clb[:, c4:c4 + 1]),
                  reads=[lnA.b(), clg.b(), clb.b()], writes=[AT.b(c4)])

        filler.append(_ln_mean)
        filler.append(_ln_sq)
        filler.append(_ln_var)
        for c4_ in range(4):
            filler.append(lambda c4_=c4_: _ln_chunk(c4_))
        stage(7)
        for g in range(2):
            if g == 1:
                if qt + 1 < NT:
                    norm_stats(xt)
                q_proj(1)
            stage(8)
            if g == 1:
                cmp_front, cmp_back = make_cmp(1)
                cmp_front(0)
                for hl in range(4):
                    if hl + 1 < 4:
                        cmp_front(hl + 1)
                    cmp_back(hl)
            stage(9)

            def topk(s):
                jt0 = 8 * qt + 2 * s
                P.add("vector", lambda e: e.tensor_tensor(out=impm[:], in0=impacc[:, s, :],
                                                          in1=addw[:, 63 - jt0:127 - jt0], op=ALU.add),
                      reads=[impacc.b(), addw.b()], writes=[impm.b()])
                P.add("vector", lambda e: e.memset(impm[:, 0:1], 1e30), reads=[impm.b()], writes=[impm.b()])
                P.add("vector", lambda e: e.max(out=m8a[:], in_=impm[:]), reads=[impm.b()], writes=[m8a.b()])
                P.add("vector", lambda e: e.match_replace(out=imp2[:], in_to_replace=m8a[:], in_values=impm[:],
                                                          imm_value=-3e38),
                      reads=[impm.b(), m8a.b()], writes=[imp2.b()])
                P.add("vector", lambda e: e.max(out=m8b[:], in_=imp2[:]), reads=[imp2.b()], writes=[m8b.b()])
                P.add("vector", lambda e: e.tensor_scalar(out=selb[:, s, 64:128], in0=impm[:], scalar1=m8b[:, 7:8],
                                                          scalar2=BIGM, op0=ALU.is_ge, op1=ALU.mult),
                      reads=[impm.b(), m8b.b()], writes=[selb.b(s)])

            def selbias():
                for s in range(4):
                    P.add("tensor", lambda e, s=s: e.transpose(out=pT[:, s, :], in_=selb[:, s, :], identity=identb[:]),
                          reads=[selb.b(s), identb.b()], writes=[pT.b()])
                for hl in range(4):
                    P.add("vector", lambda e, hl=hl: e.scalar_tensor_tensor(
                        out=Qs[64:128, hl, :].rearrange("p (s q) -> p s q", s=4), in0=pT[64:128, 0:4, :], scalar=-BIGM,
                        in1=Qc[64:128, hl, :].rearrange("p (s q) -> p s q", s=4), op0=ALU.add, op1=ALU.add),
                        reads=[pT.b(), Qc.b((hl, "b"))], writes=[Qs.b((hl, "b"))])

            stage(10)
            tasks = []
            unit = 0
            for br in (1, 0):
                for hl in range(4):
                    if br == 0:
                        sl_h = SLOPES[4 * g + hl]
                        kts = [kt for kt in range(0, 4 * qt + 4) if sl_h * (q0 - (kt * 128 + 127)) < 160.0]
                        if len(kts) % 2:
                            kts = [kts[0] - 1] + kts
                    else:
                        kts = list(range(max(0, 4 * qt - 4), 4 * qt + 4))
                    for ki, kt in enumerate(kts):
                        tasks.append((hl, br, ki, kt, len(kts), unit))
                    unit += 1
            n_win = sum(1 for t in tasks if t[1] == 1)
            LAG = 2
            tstate = {}

            pair_ctr = {"i": 0}

            def front(pi):
                i0 = 2 * pi
                hl, br, _, _, n, un = tasks[i0]
                h = 4 * g + hl
                k2 = 0
                pb2 = Pb2[pair_ctr["i"] % 4]
                pair_ctr["i"] += 1
                ptile, pviews = pPair[k2]
                Qx = Qs if br == 0 else Qw
                for half in range(2):
                    _, _, ki, kt, _, _ = tasks[i0 + half]
                    if br == 0:
                        lhs = Ks[:, g, kt * 128:(kt + 1) * 128]
                        kb = [Ks.b((g, kt // 4)), Ks.b("oh")]
                    else:
                        sl = kt % 8
                        lhs = Kw[:, g, sl * 128:(sl + 1) * 128]
                        kb = [Kw.b((g, (kt // 4) % 2)), Kw.b("oh")]
                    pv_ = pviews[half]
                    P.add("tensor", lambda e, lhs=lhs, pv_=pv_: e.matmul(pv_[:], lhsT=lhs, rhs=Qx[:, hl, :], start=True,
                                                                      stop=True),
                          reads=kb + [Qx.b(hl), Qx.b((hl, "b"))], writes=[pv_.b()])
                P.add("scalar", lambda e: e.activation(out=pb2[:], in_=ptile[:, 0:1024], func=AF.Exp,
                                                       bias=biask[:, h:h + 1]),
                      reads=[pviews[0].b(), pviews[1].b(), biask.b()], writes=[pb2.b(0), pb2.b(1)])
                for half in range(2):
                    _, _, ki, kt, _, _ = tasks[i0 + half]
                    if kt >= 4 * qt:
                        m, tri = kt - 4 * qt, trid
                    elif br == 1:
                        m, tri = kt - (4 * qt - 4), trif
                    else:
                        continue
                    for hh_ in range(2):
                        c0_ = half * 512 + (2 * m + hh_) * 64
                        blk = pb2[hh_ * 64:(hh_ + 1) * 64, c0_:c0_ + 64]
                        trv = tri[hh_ * 64:(hh_ + 1) * 64, :]
                        P.add("vector", lambda e, blk=blk, trv=trv: e.tensor_tensor(out=blk, in0=blk, in1=trv, op=ALU.mult),
                              reads=[pb2.b(half), tri.b()], writes=[pb2.b(half)])
                tstate[pi] = pb2

            def back(pj):
                pb2 = tstate.pop(pj)
                for half in range(2):
                    hl, br, ki, kt, n, un = tasks[2 * pj + half]
                    po = pO[un % 2]
                    if br == 0:
                        vap = Vs[:, kt, g, :]
                        vb = Vs.b(kt)
                    else:
                        vap = Vw[:, kt % 8, g, :]
                        vb = Vw.b(kt % 8)
                    pbh = pb2[:, half * 512:(half + 1) * 512]
                    P.add("tensor", lambda e, vap=vap, pbh=pbh, po=po, ki=ki, n=n: e.matmul(
                        po[0:65, :], lhsT=vap, rhs=pbh, start=(ki == 0), stop=(ki == n - 1)),
                        reads=[vb, pb2.b(half)], writes=[po.b()])
                    if ki == n - 1:
                        ot = OT[un % 2]
                        P.add("vector", lambda e, ot=ot, po=po: e.tensor_copy(out=ot[:], in_=po[0:65, :]),
                              reads=[po.b()], writes=[ot.b()])

            def epilogue(hl, br, un):
                h = 4 * g + hl
                ot = OT[un % 2]
                psO = pX[:, 0:260].rearrange("p (s d) -> p s d", s=4)
                for s in range(4):
                    P.add("tensor", lambda e, s=s: e.transpose(out=psO[:, s, :], in_=ot[:, s * 128:(s + 1) * 128],
                                                               identity=identf[0:65, 0:65]),
                          reads=[ot.b(), identf.b()], writes=[pX.b()])
                P.add("vector", lambda e: e.reciprocal(out=rden[:], in_=psO[:, :, 64]), reads=[pX.b()], writes=[rden.b()])
                gi = (g * 4 + hl) * 3 + 1 + br
                P.add("vector", lambda e: e.tensor_tensor(out=scg[:], in0=rden[:], in1=Gt[:, :, gi], op=ALU.mult),
                      reads=[rden.b()] + [Gt.b(s) for s in range(4)], writes=[scg.b()])
                P.add("vector", lambda e: e.tensor_tensor(
                    out=tmpc[:], in0=psO[:, :, 0:64], in1=scg[:, :].unsqueeze(2).to_broadcast([128, 4, 64]),
                    op=ALU.mult), reads=[pX.b(), scg.b()], writes=[tmpc.b()])
                if br == 1:
                    P.add("vector", lambda e: e.tensor_tensor(out=acc4[hl][:], in0=acc4[hl][:], in1=tmpc[:], op=ALU.add),
                          reads=[acc4[hl].b(), tmpc.b()], writes=[acc4[hl].b()])
                else:
                    P.add("vector", lambda e: e.tensor_tensor(out=atok[:, :, h * 64:(h + 1) * 64], in0=acc4[hl][:],
                                                              in1=tmpc[:], op=ALU.add),
                          reads=[acc4[hl].b(), tmpc.b()], writes=[atok.b(h)])

            nT = len(tasks)
            assert nT % 2 == 0 and n_win % 2 == 0
            nP = nT // 2
            LAGP = 2
            pend = {}
            for i in range(nP + LAGP + 4):
                if i in (1, 2, 3, 4):
                    topk(i - 1)
                if i == n_win // 2:
                    selbias()
                if i < nP:
                    front(i)
                j = i - LAGP
                if 0 <= j < nP:
                    back(j)
                    t_last = tasks[2 * j + 1]
                    if t_last[2] == t_last[4] - 1:
                        pend.setdefault(i + 2, []).append((t_last[0], t_last[1], t_last[5]))
                for (hl_, br_, un_) in pend.pop(i, []):
                    epilogue(hl_, br_, un_)
                if filler:
                    filler.pop(0)()
            while filler:
                filler.pop(0)()
            assert not pend and not tstate
        stage(11)
        for s in range(4):
            for c4 in range(4):
                P.add("tensor", lambda e, s=s, c4=c4: e.transpose(out=pT[:, c4, :], in_=atok[:, s, c4 * 128:(c4 + 1) * 128],
                                                                  identity=identb[:]),
                      reads=[atok.b(h) for h in range(8)] + [identb.b()], writes=[pT.b()])
            P.add("scalar", lambda e, s=s: e.copy(out=AT[:, 4:8, s * 128:(s + 1) * 128], in_=pT[:, 0:4, :]),
                  reads=[pT.b()], writes=[AT.b((4, s))])
        dbg("AT%d" % qt, AT[:], [128, 8, 512], [AT.b(c) for c in range(4)] + [AT.b((4, s)) for s in range(4)])
        stage(12)
        atb = [AT.b(c) for c in range(4)] + [AT.b((4, s)) for s in range(4)]
        P.add("sync", lambda e: e.dma_start(out=xs[:], in_=xv[qt][:, 0, :]), writes=[xs.b()], dma="xs")
        for s in range(4):
            for nh in range(2):
                pa = pA[pa_ctr["i"] % 2]
                pa_ctr["i"] += 1
                for kc in range(8):
                    P.add("tensor", lambda e, s=s, nh=nh, kc=kc, pa=pa: e.matmul(
                        pa[:], lhsT=AT[:, kc, s * 128:(s + 1) * 128], rhs=wout[:, kc, nh * 512:(nh + 1) * 512],
                        start=(kc == 0), stop=(kc == 7)), reads=atb + [wout.b()], writes=[pa.b()])
                P.add("vector", lambda e, s=s, nh=nh, pa=pa: e.tensor_tensor(
                    out=xs[:, nh * 512:(nh + 1) * 512], in0=pa[:], in1=xs[:, nh * 512:(nh + 1) * 512], op=ALU.add),
                    reads=[pa.b(), xs.b()], writes=[xs.b()])
            P.add("sync", lambda e, s=s: e.dma_start(out=x1v[qt][:, s, :], in_=xs[:]), reads=[xs.b()], dma="x1st")
            if s + 1 < 4:
                P.add("sync", lambda e, s=s: e.dma_start(out=xs[:], in_=xv[qt][:, s + 1, :]), writes=[xs.b()], dma="xs")
      except StopStage:
        pass

    P.barrier()
    A.close()
    B = ExitStack()
    P.stack = B
    w1sb = P.sb("w1sb", [128, 8, DFF], BF16)
    w2sb = P.sb("w2sb", [128, 32, D], BF16)
    fT = P.sb("fT", [128, 32, 512], BF16)
    gfb = P.sb("gfb", [128, D], F32)
    ld("sync", gfb[:], gf.partition_broadcast(128), [gfb.b()], "c_gfb")
    rl = [P.sb("rl%d" % i, [128, 512], F32) for i in range(1)]
    w1v = w_ff1.rearrange("(kc p) n -> p kc n", p=128)
    w2v = w_ff2.rearrange("(kc p) n -> p kc n", p=128)
    for kc in range(8):
        P.add("gpsimd", lambda e, kc=kc: e.dma_start(out=w1sb[:, kc, :], in_=w1v[:, kc, :]),
              writes=[w1sb.b(kc)], dma="w1_%d" % kc)
    for kc in range(32):
        P.add("gpsimd", lambda e, kc=kc: e.dma_start(out=w2sb[:, kc, :], in_=w2v[:, kc, :]),
              writes=[w2sb.b(kc)], dma="w2_%d" % (kc % 8))
    ld("sync", gbt[:], g2.partition_broadcast(128), [gbt.b()], "c_gbt2")
    ssq2 = P.sb("ssq2", [128, 4], F32)
    rstd2 = P.sb("rstd2", [128, 4], F32)
    xsb = [P.sb("xsb%d" % i, [128, D], F32) for i in range(2)]
    for s_ in range(4):
        load_x(x1v, 0, s_, "xtb")
    norm_stats(xt)
    for qt in range(NT):
        norm_apply(xt, gbt)
        if qt + 1 < NT:
            for s_ in range(4):
                load_x(x1v, qt + 1, s_, "xtb")
        for n in range(32):
            pa = pA[n % 2]
            for kc in range(8):
                P.add("tensor", lambda e, kc=kc, n=n, pa=pa: e.matmul(pa[:], lhsT=w1sb[:, kc, n * 128:(n + 1) * 128],
                                                                     rhs=hT[:, kc, :], start=(kc == 0), stop=(kc == 7)),
                      reads=hTall + [w1sb.b(kc)], writes=[pa.b()])
            r = rl[0]
            P.add("scalar", lambda e, r=r, pa=pa: e.activation(out=r[:], in_=pa[:], func=AF.Relu),
                  reads=[pa.b()], writes=[r.b()])
            P.add("vector", lambda e, r=r, n=n: e.tensor_tensor(out=fT[:, n, :], in0=r[:], in1=r[:], op=ALU.mult),
                  reads=[r.b()], writes=[fT.b(n)])
        if qt + 1 < NT:
            norm_stats(xt)
        fall = [fT.b(n) for n in range(32)]
        for s in range(4):
            xc = xsb[s % 2]
            P.add("sync", lambda e, s=s, xc=xc: e.dma_start(out=xc[:], in_=x1v[qt][:, s, :]), writes=[xc.b()],
                  dma="xsb%d" % (s % 2))
            for nh in range(2):
                pa = pS[nh]
                for n in range(32):
                    P.add("tensor", lambda e, n=n, s=s, nh=nh, pa=pa: e.matmul(
                        pa[:], lhsT=fT[:, n, s * 128:(s + 1) * 128], rhs=w2sb[:, n, nh * 512:(nh + 1) * 512],
                        start=(n == 0), stop=(n == 31)),
                        reads=fall + [w2sb.b(n)], writes=[pa.b()])
                P.add("vector", lambda e, nh=nh, pa=pa, xc=xc: e.tensor_tensor(
                    out=xc[:, nh * 512:(nh + 1) * 512], in0=pa[:], in1=xc[:, nh * 512:(nh + 1) * 512], op=ALU.add),
                    reads=[pa.b(), xc.b()], writes=[xc.b()])
            P.add("scalar", lambda e, s=s, xc=xc: e.activation(out=junk[:], in_=xc[:], func=AF.Square,
                                                               accum_out=ssq2[:, s:s + 1]),
                  reads=[xc.b()], writes=[junk.b(), ssq2.b(s)])
            P.add("scalar", lambda e, s=s: e.activation(out=rstd2[:, s:s + 1], in_=ssq2[:, s:s + 1], func=AF.Sqrt,
                                                        scale=1.0 / D, bias=EPS),
                  reads=[ssq2.b(s)], writes=[rstd2.b(s)])
            P.add("vector", lambda e, s=s: e.reciprocal(out=rstd2[:, s:s + 1], in_=rstd2[:, s:s + 1]),
                  reads=[rstd2.b(s)], writes=[rstd2.b(s)])
            P.add("vector", lambda e, s=s, xc=xc: e.scalar_tensor_tensor(out=xc[:], in0=xc[:], scalar=rstd2[:, s:s + 1],
                                                                        in1=gfb[:], op0=ALU.mult, op1=ALU.mult),
                  reads=[xc.b(), rstd2.b(s), gfb.b()], writes=[xc.b()])
            P.add("sync", lambda e, s=s, xc=xc: e.dma_start(out=outv[qt][:, s, :], in_=xc[:]), reads=[xc.b()],
                  dma="ost%d" % (s % 2))
    P.stack = outer_stack
    P.emit()
    P._phase_stacks = (A, B)
    return nc, P, dbg_out


_CACHE = {}


def _get_prog(T):
    if T not in _CACHE:
        _CACHE[T] = build(T)
    return _CACHE[T]


def make_inputs_for_core(xb, wl, consts, T):
    m = dict(wl)
    m.update(consts)
    m["x"] = np.ascontiguousarray(xb, dtype=np.float32)
    ohk = np.zeros((64, T), np.float32)
    for j in range(T // 64):
        ohk[j, j * 64:(j + 1) * 64] = 1
    m["ohk"] = ohk
    return m


def kernel(**inputs):
    x = np.asarray(inputs["x"], np.float32)
    Bn, T, _ = x.shape
    wl = layout_weights(inputs)
    consts = make_consts()
    nc, P, _ = _get_prog(T)
    in_maps = [make_inputs_for_core(x[b], wl, consts, T) for b in range(Bn)]
    res = run_bass_kernel_spmd(nc, in_maps, core_ids=list(range(Bn)))
    return np.stack([np.asarray(r["out"], np.float32) for r in res.results], axis=0)
```

```python
from contextlib import ExitStack
import numpy as np
import ml_dtypes
import concourse.bass as bass
import concourse.mybir as mybir
from concourse.bass_utils import run_bass_kernel_spmd

F32 = mybir.dt.float32
BF16 = mybir.dt.bfloat16
AF = mybir.ActivationFunctionType
ALU = mybir.AluOpType
AX = mybir.AxisListType

ENGS = ["tensor", "scalar", "vector", "gpsimd", "sync"]

D = 1024
DFF = 4096
DIN = 2328
EPS = 1e-6
BIGM = 32768.0
SLOPES = [2.0 ** (-(h + 1)) for h in range(8)]
USE_GELU_TANH = True


import types


def _freeze(fn):
    if fn.__closure__ is None:
        return fn
    cells = []
    for c in fn.__closure__:
        try:
            cells.append(types.CellType(c.cell_contents))
        except ValueError:
            cells.append(c)
    return types.FunctionType(fn.__code__, fn.__globals__, fn.__name__, fn.__defaults__, tuple(cells))

class Buf:
    __slots__ = ("name", "lw", "rd", "excl")

    def __init__(self, name, excl=False):
        self.name = name
        self.lw = None
        self.rd = []
        self.excl = excl


class Op:
    __slots__ = ("eng", "fn", "reads", "writes", "dma", "deps", "idx", "sig", "done", "waits")


class Tile:
    def __init__(self, h, name, psum=False):
        self.h = h
        self.name = name
        self.bufs = {}
        self.psum = psum

    def b(self, key=None):
        if self.psum:
            key = None
        if key not in self.bufs:
            self.bufs[key] = Buf("%s:%s" % (self.name, key), excl=self.psum)
        return self.bufs[key]

    def __getitem__(self, idx):
        return self.h[idx]


class Prog:
    def __init__(self, nc):
        self.nc = nc
        self.ops = []
        self.stack = ExitStack()
        self.barrier_at = []

    def sb(self, name, shape, dt):
        h = self.stack.enter_context(self.nc.sbuf_tensor("s_" + name, list(shape), dt))
        return Tile(h, name)

    def ps(self, name, shape, dt):
        h = self.stack.enter_context(self.nc.psum_tensor("p_" + name, list(shape), dt))
        return Tile(h, name, psum=True)

    def add(self, eng, fn, reads=(), writes=(), dma=None):
        o = Op()
        o.eng = eng
        o.fn = _freeze(fn)
        o.reads = [r for r in reads if r is not None and not r.excl]
        o.writes = [w for w in writes if w is not None] + [r for r in reads if r is not None and r.excl]
        o.dma = dma
        o.idx = len(self.ops)
        o.sig = dma is not None
        o.deps = set()
        o.waits = []
        self.ops.append(o)
        return o

    def barrier(self):
        self.barrier_at.append(len(self.ops))

    def analyze(self):
        ops = self.ops
        last_on_eng = {}
        barrier_set = set(self.barrier_at)
        pending_barrier = None
        seen_after_barrier = set()
        for o in ops:
            if o.idx in barrier_set:
                pending_barrier = dict(last_on_eng)
                seen_after_barrier = set()
            deps = set()
            for b in o.reads:
                if b.lw is not None:
                    deps.add(b.lw)
            for b in o.writes:
                if b.lw is not None:
                    deps.add(b.lw)
                for r in b.rd:
                    deps.add(r)
            if pending_barrier is not None and o.eng not in seen_after_barrier:
                seen_after_barrier.add(o.eng)
                for e, i in pending_barrier.items():
                    deps.add(i)
            for b in o.reads:
                b.rd.append(o.idx)
            for b in o.writes:
                b.lw = o.idx
                b.rd = []
            fin = set()
            for d in deps:
                if d == o.idx:
                    continue
                y = ops[d]
                if y.dma is None and y.eng == o.eng:
                    if o.eng in ("tensor", "sync"):
                        continue
                    yw = set(id(b) for b in y.writes)
                    touched = set(id(b) for b in o.reads) | set(id(b) for b in o.writes)
                    if not (yw & touched):
                        continue
                fin.add(d)
            o.deps = fin
            for d in fin:
                ops[d].sig = True
            last_on_eng[o.eng] = o.idx
            if o.dma is not None:
                last_on_eng['D_' + o.dma] = o.idx
        cnt = {}
        for o in ops:
            if o.dma is not None:
                k = "D_" + o.dma
                cnt[k] = cnt.get(k, 0) + 16
                o.done = (k, cnt[k])
            elif o.sig:
                k = "E_" + o.eng
                cnt[k] = cnt.get(k, 0) + 1
                o.done = (k, cnt[k])
            else:
                o.done = None
        self.final_counts = dict(cnt)
        known = {e: {} for e in ENGS}
        for o in ops:
            need = {}
            for d in o.deps:
                k, v = ops[d].done
                if v > need.get(k, 0):
                    need[k] = v
            kn = known[o.eng]
            o.waits = []
            for k, v in need.items():
                if kn.get(k, 0) >= v:
                    continue
                kn[k] = v
                o.waits.append((k, v))

    def emit(self, final_eng="sync"):
        self.analyze()
        nc = self.nc
        sems = {}
        for k in self.final_counts:
            sems[k] = self.stack.enter_context(nc.semaphore(k))
        self.nsem = len(sems)
        final_counts = self.final_counts
        with nc.Block() as block:
            for en in ENGS:
                eops = [o for o in self.ops if o.eng == en]

                def body(e, eops=eops, en=en):
                    for o in eops:
                        for (k, v) in o.waits:
                            e.wait_ge(sems[k], v)
                        ins = o.fn(e)
                        if o.sig:
                            ins.then_inc(sems[o.done[0]], 16 if o.dma is not None else 1)
                    if en == final_eng:
                        for k, v in final_counts.items():
                            if k.startswith("D_"):
                                e.wait_ge(sems[k], v)

                getattr(block, en)(body)


def make_consts():
    c = {}
    c["ident"] = np.eye(128, dtype=np.float32)
    p = np.arange(128)
    a = np.arange(512)
    a_hi, a_lo = a // 64, a % 64
    t0 = np.zeros((128, 512), np.float32)
    for j in range(64):
        t0[64 + j] = a_hi - j
    c["t0"] = t0
    dr = np.zeros((2, 128, 512), np.float32)
    for par in range(2):
        for jr in range(16):
            dr[par, 64 + jr] = (8 * par + a_hi - jr) % 16
    c["dr"] = dr
    oh16 = np.zeros((16, 1024), np.float32)
    for jr in range(16):
        oh16[jr, jr * 64:(jr + 1) * 64] = 1
    c["oh16"] = oh16
    ohc = np.zeros((64, 256), np.float32)
    for cp in range(256):
        ohc[cp // 4, cp] = 1
    c["ohc"] = ohc
    sl = np.array(SLOPES, np.float32)
    c["biask"] = ((p % 64)[:, None] * sl[None, :]).astype(np.float32)
    bc = np.zeros((128, 2, 8), np.float32)
    for ct in range(2):
        cp = ct * 128 + p
        bc[:, ct, :] = (16 * (cp % 4) + 15)[:, None] * sl[None, :]
    bc[0, 0, :] = -30000.0
    c["biasc"] = bc
    ova = np.zeros((128, 2, 65), np.float32)
    for ct in range(2):
        for pp in range(128):
            cp = ct * 128 + pp
            if cp == 0:
                continue
            cc = cp - 1
            for j in range(64):
                if (16 * cc < 64 * (j + 1)) and (16 * cc + 32 > 64 * j):
                    ova[pp, ct, j] = 1
    ova[:, :, 64] = 1
    c["ova"] = ova
    mc = np.ones((128, 4, 512), np.float32)
    for r in range(4):
        for pl in range(32):
            aa, m = pl // 4, pl % 4
            row = np.where(a_hi > aa, 1.0, np.where(a_hi == aa, (a_lo >= 16 * m + 15) * 1.0, 0.0))
            mc[32 * r + pl, r] = row
    c["mcf"] = mc
    a64 = np.arange(64)
    c["trid"] = (((p % 64)[:, None]) <= a64[None, :]).astype(np.float32)
    c["trif"] = (((p % 64)[:, None]) > a64[None, :]).astype(np.float32)
    addw = np.zeros((128, 128), np.float32)
    for pp in range(128):
        y = np.arange(128) - (pp >= 64)
        addw[pp] = np.where((y == 62) | (y == 63), 1e30, np.where(y > 63, -1e30, 0.0))
    c["addw"] = addw
    c["onesf"] = np.full((128, 128), 1.0 / 512, np.float32)
    return c


def layout_weights(inp):
    w = {}
    sq = lambda a: np.ascontiguousarray(np.asarray(a, np.float32).reshape(np.asarray(a).shape[1:]))
    w["g1"] = sq(inp["norm1_g"])
    w["g2"] = sq(inp["norm2_g"])
    w["gf"] = np.ascontiguousarray(np.asarray(inp["norm_f_g"], np.float32))
    w["w_in"] = sq(inp["w_in"])
    w["w_out"] = sq(inp["w_out"])
    w["w_ff1"] = sq(inp["w_ff1"])
    w["w_ff2"] = sq(inp["w_ff2"])
    dw = sq(inp["dw_w"]).reshape(31, 512)
    w["dww"] = np.ascontiguousarray(dw.T.reshape(4, 128, 31).transpose(1, 0, 2))
    for nm in ("dw_b", "cln_g", "cln_b"):
        w[nm] = np.ascontiguousarray(sq(inp[nm]).reshape(4, 128).T)
    w1c = np.zeros((2, 128, 16, 256), np.float32)
    pe2 = np.zeros((128, 2, 16), np.float32)
    w2c = np.zeros((2, 128, 2, 64), np.float32)
    for kv, (nw1, npe, nw2) in enumerate((("ck_w1", "ck_pe", "ck_w2"), ("cv_w1", "cv_pe", "cv_w2"))):
        a1 = sq(inp[nw1]).reshape(32, 64, 256)
        w1c[kv, 0:64] = a1[0:16].transpose(1, 0, 2)
        w1c[kv, 64:128] = a1[16:32].transpose(1, 0, 2)
        pe = sq(inp[npe])
        pe2[0:64, kv, :] = pe[0:16].T
        pe2[64:128, kv, :] = pe[16:32].T
        w2c[kv] = sq(inp[nw2]).reshape(2, 128, 64).transpose(1, 0, 2)
    w["w1c"] = w1c
    w["pe2"] = pe2
    w["w2c"] = w2c
    return w


class StopStage(Exception):
    pass


def build(T, debug=(), stop=99):
    NT = T // 512

    def stage(n):
        if n > stop:
            raise StopStage()
    nc = bass.Bass("TRN2", target_bir_lowering=False)
    P = Prog(nc)
    dram_in = {}

    def din(name, shape):
        dram_in[name] = nc.dram_tensor(name, list(shape), F32, kind="ExternalInput").ap()
        return dram_in[name]

    x = din("x", [T, D])
    g1 = din("g1", [D]); g2 = din("g2", [D]); gf = din("gf", [D])
    w_in = din("w_in", [D, DIN]); w_out = din("w_out", [D, D])
    w_ff1 = din("w_ff1", [D, DFF]); w_ff2 = din("w_ff2", [DFF, D])
    dww_d = din("dww", [128, 4, 31]); dwb_d = din("dw_b", [128, 4]); clg_d = din("cln_g", [128, 4]); clb_d = din("cln_b", [128, 4])
    w1c_d = din("w1c", [2, 128, 16, 256]); pe2_d = din("pe2", [128, 2, 16]); w2c_d = din("w2c", [2, 128, 2, 64])
    ident_d = din("ident", [128, 128]); t0_d = din("t0", [128, 512]); dr_d = din("dr", [2, 128, 512])
    oh16_d = din("oh16", [16, 1024]); ohc_d = din("ohc", [64, 256]); ohk_d = din("ohk", [64, T])
    biask_d = din("biask", [128, 8]); biasc_d = din("biasc", [128, 2, 8]); ova_d = din("ova", [128, 2, 65])
    mcf_d = din("mcf", [128, 4, 512]); trid_d = din("trid", [128, 64]); trif_d = din("trif", [128, 64])
    addw_d = din("addw", [128, 128]); onesf_d = din("onesf", [128, 128])
    out = nc.dram_tensor("out", [T, D], F32, kind="ExternalOutput").ap()
    x1d = nc.dram_tensor("x1d", [T, D], F32).ap()
    dbg_out = {}

    O_UV, O_UG, O_Q, O_KV, O_G = 0, 512, 1024, 1536, 2304
    kvcol = lambda i: O_KV + i * 128

    pT = P.ps("pT", [128, 8, 128], BF16)
    class PView:
        def __init__(self, ap, name):
            self.ap = ap
            self._b = Buf(name, excl=True)

        def b(self, key=None):
            return self._b

        def __getitem__(self, idx):
            return self.ap[idx]

    pAA = P.ps("pAA", [128, 1024], F32)
    pSS = P.ps("pSS", [128, 1024], F32)
    pA = [PView(pAA.h[:, i * 512:(i + 1) * 512], "pA%d" % i) for i in range(2)]
    pS = [PView(pSS.h[:, i * 512:(i + 1) * 512], "pS%d" % i) for i in range(2)]
    pPair = [(pSS.h, pS), (pAA.h, pA)]
    pO = [P.ps("pO%d" % i, [128, 512], F32) for i in range(2)]
    pX = P.ps("pX", [128, 512], F32)

    cst = ExitStack()
    identb = P.sb("identb", [128, 128], BF16)
    identf = P.sb("identf", [128, 128], F32)
    gbt = P.sb("gbt", [128, D], F32)
    xt = P.sb("xt", [128, 4, D], F32)
    ssq = P.sb("ssq", [128, 4], F32)
    rstd = P.sb("rstd", [128, 4], F32)
    htok = [P.sb("htok%d" % i, [128, D], BF16) for i in range(2)]
    junk = htok[1]
    hT = P.sb("hT", [128, 8, 512], BF16)

    def ld(eng, dst_ap, src_ap, wbufs, key):
        P.add(eng, lambda e: e.dma_start(out=dst_ap, in_=src_ap), writes=wbufs, dma=key)

    ld("gpsimd", identb[:], ident_d, [identb.b()], "c_identb")
    ld("sync", identf[:], ident_d, [identf.b()], "c_identf")

    def norm_stats(xc):
        for s in range(4):
            P.add("scalar", lambda e, s=s: e.activation(out=junk[:], in_=xc[:, s, :], func=AF.Square,
                                                        accum_out=ssq[:, s:s + 1]),
                  reads=[xc.b(s)], writes=[junk.b(), ssq.b(s)])
            P.add("scalar", lambda e, s=s: e.activation(out=rstd[:, s:s + 1], in_=ssq[:, s:s + 1], func=AF.Sqrt,
                                                        scale=1.0 / D, bias=EPS),
                  reads=[ssq.b(s)], writes=[rstd.b(s)])
            P.add("vector", lambda e, s=s: e.reciprocal(out=rstd[:, s:s + 1], in_=rstd[:, s:s + 1]),
                  reads=[rstd.b(s)], writes=[rstd.b(s)])

    def norm_apply(xc, gtile):
        for s in range(4):
            ht = htok[s % 2]
            P.add("vector", lambda e, s=s, ht=ht: e.scalar_tensor_tensor(
                out=ht[:], in0=xc[:, s, :], scalar=rstd[:, s:s + 1], in1=gtile[:], op0=ALU.mult, op1=ALU.mult),
                reads=[xc.b(s), rstd.b(s), gtile.b()], writes=[ht.b()])
            for kc in range(8):
                P.add("tensor", lambda e, kc=kc, ht=ht: e.transpose(out=pT[:, kc, :], in_=ht[:, kc * 128:(kc + 1) * 128],
                                                                   identity=identb[:]),
                      reads=[ht.b(), identb.b()], writes=[pT.b()])
            P.add("vector" if s % 2 else "scalar", lambda e, s=s: (e.tensor_copy if s % 2 else e.copy)(
                out=hT[:, :, s * 128:(s + 1) * 128], in_=pT[:]),
                reads=[pT.b()], writes=[hT.b(s)])

    def load_x(src_v, qt_, s, key, xc=None):
        xc = xc if xc is not None else xt
        P.add("sync", lambda e: e.dma_start(out=xc[:, s, :], in_=src_v[qt_][:, s, :]), writes=[xc.b(s)],
              dma="%s%d" % (key, s))

    hTall = [hT.b(s) for s in range(4)]

    def dbg(name, src_ap, shape, rbufs):
        if name not in debug:
            return
        d = nc.dram_tensor("dbg_" + name, list(shape), src_ap.dtype, kind="ExternalOutput").ap()
        dbg_out[name] = d
        P.add("sync", lambda e: e.dma_start(out=d, in_=src_ap), reads=rbufs, dma="dbg_" + name)

    A = ExitStack()
    P.stack, outer_stack = A, P.stack
    wv = P.sb("wv", [128, 8, 280], BF16)
    wout = P.sb("wout", [128, 8, D], BF16)
    w1c = P.sb("w1c", [128, 2, 16, 256], BF16)
    w2c = P.sb("w2c", [128, 2, 2, 64], BF16)
    pe2 = P.sb("pe2", [128, 2, 16], BF16)
    b1 = P.sb("b1", [128, 2, 2], F32)
    dww = P.sb("dww", [128, 4, 31], F32)
    dwb = P.sb("dwb", [128, 4], F32); clg = P.sb("clg", [128, 4], F32); clb = P.sb("clb", [128, 4], F32)
    t0 = P.sb("t0", [128, 512], BF16)
    drt = P.sb("drt", [128, 2, 512], BF16)
    biask = P.sb("biask", [128, 8], F32); biasc = P.sb("biasc", [128, 2, 8], F32)
    ova = P.sb("ova", [128, 2, 65], BF16)
    mcf = P.sb("mcf", [128, 4, 512], BF16); trid = P.sb("trid", [128, 64], BF16); trif = P.sb("trif", [128, 64], BF16)
    addw = P.sb("addw", [128, 128], F32)
    onesf = P.sb("onesf", [128, 128], F32)
    Ks = P.sb("Ks", [128, 2, T], BF16)
    Kw = P.sb("Kw", [128, 2, 1024], BF16)
    Vs = P.sb("Vs", [128, T // 128, 2, 65], BF16)
    Vw = P.sb("Vw", [128, 8, 2, 65], BF16)
    Kc = P.sb("Kc", [128, 2, 256], BF16)
    vcT = P.sb("vcT", [64, 2, 256], BF16)
    vcs = P.sb("vcs", [128, 2, 2, 64], BF16)
    wb = [P.sb("wb%d" % i, [128, 8, 128], BF16) for i in range(3)]
    xs = P.sb("xs", [128, D], F32)
    dgbA = P.sb("dgbA", [128, 16, 128], BF16)
    dgbB = P.sb("dgbB", [128, 15, 128], BF16)
    Gt = P.sb("Gt", [128, 4, 24], F32)
    ub = [P.sb("ub%d" % i, [128, 4, 542], BF16) for i in range(1)]
    sg = P.sb("sg", [128, 512], BF16)
    yb = P.sb("yb", [128, 4, 512], F32)
    lnA = P.sb("lnA", [128, 512], F32); lnB = P.sb("lnB", [128, 512], F32); lnC = P.sb("lnC", [128, 512], F32)
    AT = P.sb("AT", [128, 8, 512], BF16)
    KK = [P.sb("KK%d" % i, [128, 4, 544], BF16) for i in range(1)]
    hid = P.sb("hid", [128, 2, 2, 64], BF16)
    hx = P.sb("hx", [128, 64], F32); hy = P.sb("hy", [128, 64], F32)
    Qc = P.sb("Qc", [128, 4, 512], BF16); Qs = P.sb("Qs", [128, 4, 512], BF16); Qw = P.sb("Qw", [128, 4, 512], BF16)
    Dq = P.sb("Dq", [128, 512], BF16); NEGM = P.sb("NEGM", [128, 512], BF16); WBR = P.sb("WBR", [128, 512], BF16)
    Pb2 = [P.sb("Pb2_%d" % i, [128, 1024], BF16) for i in range(4)]

    class SView:
        def __init__(self, ap, buf):
            self.ap = ap
            self._b = buf

        def b(self, key=None):
            return self._b

        def __getitem__(self, idx):
            return self.ap[idx]

    Pb = [SView(Pb2[i // 2].h[:, (i % 2) * 512:(i % 2 + 1) * 512], Pb2[i // 2].b(i % 2)) for i in range(8)]
    OT = [P.sb("OT%d" % i, [65, 512], F32) for i in range(2)]
    acc4 = [P.sb("acc%d" % i, [128, 4, 64], F32) for i in range(4)]
    tmpc = P.sb("tmpc", [128, 4, 64], F32)
    atok = P.sb("atok", [128, 4, 512], BF16)
    impacc = P.sb("impacc", [128, 4, 64], F32)
    impm = P.sb("impm", [128, 64], F32); imp2 = P.sb("imp2", [128, 64], F32)
    m8a = P.sb("m8a", [128, 8], F32); m8b = P.sb("m8b", [128, 8], F32)
    selb = P.sb("selb", [128, 4, 128], BF16)
    rden = P.sb("rden", [128, 4], F32); scg = P.sb("scg", [128, 4], F32)

    cstq = ["sync", "gpsimd"]
    ld("sync", gbt[:], g1.partition_broadcast(128), [gbt.b()], "c_gbt")
    for kc in range(8):
        wiv = w_in[kc * 128:(kc + 1) * 128, :]
        ld("gpsimd", wv[:, kc, 0:128], wiv[:, kvcol(3):kvcol(3) + 128], [wv.b()], "c_wv")
        ld("gpsimd", wv[:, kc, 128:256], wiv[:, kvcol(5):kvcol(5) + 128], [wv.b()], "c_wv")
        ld("gpsimd", wv[:, kc, 256:280], wiv[:, O_G:O_G + 24], [wv.b()], "c_wv")
    ld("gpsimd", wout[:], w_out.rearrange("(kc p) n -> p kc n", p=128), [wout.b()], "c_wout")
    for kv in range(2):
        ld("gpsimd", w1c[:, kv], w1c_d[kv], [w1c.b()], "c_w1c")
        ld("gpsimd", w2c[:, kv], w2c_d[kv], [w2c.b()], "c_w2c")
    ld("gpsimd", pe2[:], pe2_d, [pe2.b()], "c_pe2")
    for (tl, dd, k) in ((dww, dww_d, "dww"), (dwb, dwb_d, "dwb"), (clg, clg_d, "clg"), (clb, clb_d, "clb"),
                        (biask, biask_d, "biask"), (biasc, biasc_d, "biasc"), (addw, addw_d, "addw"),
                        (onesf, onesf_d, "onesf")):
        ld("sync", tl[:], dd, [tl.b()], "c_" + k)
    for (tl, dd, k) in ((t0, t0_d, "t0"), (ova, ova_d, "ova"), (mcf, mcf_d, "mcf"), (trid, trid_d, "trid"),
                        (trif, trif_d, "trif")):
        ld("gpsimd", tl[:], dd, [tl.b()], "c_" + k)
    for par in range(2):
        ld("gpsimd", drt[:, par, :], dr_d[par], [drt.b()], "c_drt")
    P.add("vector", lambda e: e.memset(Ks[:], 0.0), writes=[Ks.b("z")])
    P.add("vector", lambda e: e.memset(Kw[:], 0.0), writes=[Kw.b("z")])
    P.add("vector", lambda e: e.memset(Kc[:], 0.0), writes=[Kc.b("z")])
    P.add("vector", lambda e: e.memset(vcT[:], 0.0), writes=[vcT.b()])
    P.add("vector", lambda e: e.memset(vcs[:], 0.0), writes=[vcs.b()])
    P.add("vector", lambda e: e.memset(Vs[:], 1.0), writes=[Vs.b("z")])
    P.add("vector", lambda e: e.memset(Vw[:], 1.0), writes=[Vw.b("z")])
    P.add("vector", lambda e: e.memset(Qw[:], 0.0), writes=[Qw.b("z")])
    P.add("vector", lambda e: e.memset(selb[:], 0.0), writes=[selb.b(s) for s in range(4)])
    for i in range(1):
        P.add("vector", lambda e, i=i: e.memset(KK[i][:], 0.0), writes=[KK[i].b()])
        P.add("vector", lambda e, i=i: e.memset(ub[i][:], 0.0), writes=[ub[i].b(c) for c in range(4)])
    for g in range(2):
        P.add("gpsimd", lambda e, g=g: e.dma_start(out=Ks[64:128, g, :], in_=ohk_d), reads=[Ks.b("z")],
              writes=[Ks.b("oh")], dma="c_ohk")
        P.add("gpsimd", lambda e, g=g: e.dma_start(out=Kw[64:80, g, :], in_=oh16_d), reads=[Kw.b("z")],
              writes=[Kw.b("oh")], dma="c_oh16")
        P.add("gpsimd", lambda e, g=g: e.dma_start(out=Kc[64:128, g, :], in_=ohc_d), reads=[Kc.b("z")],
              writes=[Kc.b("oh")], dma="c_ohc")
    for kv in range(2):
        for mc in range(2):
            for l in range(16):
                P.add("tensor", lambda e, kv=kv, mc=mc, l=l: e.matmul(
                    pX[:, kv * 2 + mc:kv * 2 + mc + 1], lhsT=w1c[:, kv, l, mc * 128:(mc + 1) * 128],
                    rhs=pe2[:, kv, l:l + 1], start=(l == 0), stop=(l == 15)),
                    reads=[w1c.b(), pe2.b()], writes=[pX.b()])
    P.add("vector", lambda e: e.tensor_copy(out=b1[:], in_=pX[:, 0:4].rearrange("p (a b) -> p a b", a=2)),
          reads=[pX.b()], writes=[b1.b()])

    wreq = []
    for qt in range(NT):
        wreq.append([(0, 128, kvcol(2))])
        wreq.append([(0, 128, kvcol(4))])
        for i in (0, 1):
            for g in range(2):
                wreq.append([(0, 64, kvcol(i) + g * 64), (64, 64, kvcol(i) + g * 64)])
        for c4 in range(4):
            wreq.append([(0, 128, O_UG + c4 * 128)])
            wreq.append([(0, 128, O_UV + c4 * 128)])
        for hp in range(4):
            wreq.append([(0, 128, O_Q + hp * 128)])
    wstate = {"issued": 0}
    w_in_v = w_in.rearrange("(kc p) n -> p kc n", p=128)

    NGRP = len(wreq) // NT
    wsc = nc.dram_tensor("wsc", [NGRP, 128, 1024], BF16).ap()
    wsc_b = [Buf("wsc%d" % i) for i in range(NGRP)]

    def w_ensure(upto):
        while wstate["issued"] <= min(upto, len(wreq) - 1):
            i = wstate["issued"]
            t = wb[i % 3]
            if i < NGRP:
                for (c0, ncol, s0) in wreq[i]:
                    P.add("gpsimd", lambda e, t=t, c0=c0, ncol=ncol, s0=s0: e.dma_start(
                        out=t[:, :, c0:c0 + ncol], in_=w_in_v[:, :, s0:s0 + ncol]),
                        writes=[t.b()], dma="wbg%d" % (i % 3))
                P.add("sync", lambda e, t=t, i=i: e.dma_start(out=wsc[i], in_=t[:].rearrange("p a b -> p (a b)")),
                      reads=[t.b()], writes=[wsc_b[i % 3]], dma="wsc%d" % (i % 3))
            else:
                gi = i % NGRP
                P.add("sync", lambda e, t=t, gi=gi: e.dma_start(out=t[:].rearrange("p a b -> p (a b)"), in_=wsc[gi]),
                      reads=[wsc_b[gi % 3]], writes=[t.b()], dma="wbs%d" % (i % 3))
            wstate["issued"] += 1

    wctr = {"i": 0}

    def next_w():
        i = wctr["i"]
        w_ensure(i + 1)
        wctr["i"] += 1
        return wb[i % 3]

    pa_ctr = {"i": 0}

    def fm_group(wt, c0, M):
        pa = pA[pa_ctr["i"] % 2]
        pa_ctr["i"] += 1
        for kc in range(8):
            P.add("tensor", lambda e, kc=kc, pa=pa: e.matmul(pa[0:M, :], lhsT=wt[:, kc, c0:c0 + M], rhs=hT[:, kc, :],
                                                            start=(kc == 0), stop=(kc == 7)),
                  reads=hTall + [wt.b()], writes=[pa.b()])
        return pa

    dg_ctr = {"i": 0}
    ps_ctr = {"i": 0}
    pb_ctr = {"i": 0}

    xv = x.rearrange("(q s p) d -> q p s d", p=128, s=4)
    x1v = x1d.rearrange("(q s p) d -> q p s d", p=128, s=4)
    outv = out.rearrange("(q s p) d -> q p s d", p=128, s=4)

    for qt in range(NT):
      try:
        par = qt % 2
        q0 = qt * 512
        ctm = qt // 4
        if qt == 0:
            for s_ in range(4):
                load_x(xv, 0, s_, "xt")
            norm_stats(xt)
        norm_apply(xt, gbt)
        if qt + 1 < NT:
            for s_ in range(4):
                load_x(xv, qt + 1, s_, "xt")
        stage(1)
        for s in range(4):
            kt = 4 * qt + s
            for kc in range(8):
                P.add("tensor", lambda e, kc=kc, s=s: e.matmul(pX[:, 0:280], lhsT=hT[:, kc, s * 128:(s + 1) * 128],
                                                               rhs=wv[:, kc, :], start=(kc == 0), stop=(kc == 7)),
                      reads=[hT.b(s), wv.b()], writes=[pX.b()])
            P.add("scalar", lambda e, kt=kt: e.copy(out=Vs[:, kt, :, 0:64],
                                                    in_=pX[:, 0:128].rearrange("p (g d) -> p g d", g=2)),
                  reads=[pX.b(), Vs.b("z")], writes=[Vs.b(kt)])
            P.add("vector", lambda e, kt=kt: e.tensor_copy(out=Vw[:, kt % 8, :, 0:64],
                                                           in_=pX[:, 128:256].rearrange("p (g d) -> p g d", g=2)),
                  reads=[pX.b(), Vw.b("z")], writes=[Vw.b(kt % 8)])
            P.add("scalar", lambda e, s=s: e.activation(out=Gt[:, s, :], in_=pX[:, 256:280], func=AF.Sigmoid),
                  reads=[pX.b()], writes=[Gt.b(s)])
        stage(2)
        wt = next_w()
        for g in range(2):
            pa = fm_group(wt, g * 64, 64)
            P.add("scalar", lambda e, g=g, pa=pa: e.copy(out=Ks[0:64, g, q0:q0 + 512], in_=pa[0:64, :]),
                  reads=[pa.b(), Ks.b("z")], writes=[Ks.b((g, qt))])
        wt = next_w()
        r0 = (qt % 2) * 512
        for g in range(2):
            pa = fm_group(wt, g * 64, 64)
            P.add("scalar", lambda e, g=g, pa=pa: e.copy(out=Kw[0:64, g, r0:r0 + 512], in_=pa[0:64, :]),
                  reads=[pa.b(), Kw.b("z")], writes=[Kw.b((g, qt % 2))])
        stage(3)
        kk = KK[0]
        kko = KK[0]
        if qt > 0:
            P.add("gpsimd", lambda e, kk=kk, kko=kko: e.tensor_copy(out=kk[0:64, :, 0:32], in_=kko[0:64, :, 512:544]),
                  reads=[kko.b()], writes=[kk.b()])
            P.add("gpsimd", lambda e, kk=kk, kko=kko: e.tensor_copy(out=kk[64:128, :, 0:16], in_=kko[64:128, :, 512:528]),
                  reads=[kko.b()], writes=[kk.b()])
        for i in (0, 1):
            for g in range(2):
                wt = next_w()
                pa = fm_group(wt, 0, 128)
                P.add("scalar", lambda e, i=i, g=g, pa=pa, kk=kk: e.copy(out=kk[0:64, 2 * i + g, 32:544], in_=pa[0:64, :]),
                      reads=[pa.b()], writes=[kk.b()])
                P.add("vector", lambda e, i=i, g=g, pa=pa, kk=kk: e.tensor_copy(out=kk[64:128, 2 * i + g, 16:528],
                                                                               in_=pa[64:128, :]),
                      reads=[pa.b()], writes=[kk.b()])
        stage(4)
        for kv in range(2):
            for mc in range(2):
                for l in range(16):
                    P.add("tensor", lambda e, kv=kv, mc=mc, l=l, kk=kk: e.matmul(
                        pX[:, 0:64], lhsT=w1c[:, kv, l, mc * 128:(mc + 1) * 128],
                        rhs=kk[:, 2 * kv:2 * kv + 2, 16 + l:16 + l + 512:16], start=(l == 0), stop=(l == 15)),
                        reads=[w1c.b(), kk.b()], writes=[pX.b()])
                if USE_GELU_TANH:
                    P.add("scalar", lambda e, kv=kv, mc=mc: e.activation(
                        out=hid[:, kv, mc, :], in_=pX[:, 0:64], func=AF.Gelu_apprx_tanh, bias=b1[:, kv, mc:mc + 1]),
                        reads=[pX.b(), b1.b()], writes=[hid.b()])
                else:
                    P.add("scalar", lambda e, kv=kv, mc=mc: e.activation(
                        out=hx[:], in_=pX[:, 0:64], func=AF.Identity, bias=b1[:, kv, mc:mc + 1]),
                        reads=[pX.b(), b1.b()], writes=[hx.b()])
                    P.add("vector", lambda e: e.tensor_tensor(out=hy[:], in0=hx[:], in1=hx[:], op=ALU.mult),
                          reads=[hx.b()], writes=[hy.b()])
                    P.add("vector", lambda e: e.tensor_scalar(out=hy[:], in0=hy[:], scalar1=0.044715, scalar2=1.0,
                                                              op0=ALU.mult, op1=ALU.add),
                          reads=[hy.b()], writes=[hy.b()])
                    P.add("vector", lambda e: e.tensor_tensor(out=hy[:], in0=hy[:], in1=hx[:], op=ALU.mult),
                          reads=[hy.b(), hx.b()], writes=[hy.b()])
                    P.add("scalar", lambda e: e.activation(out=hy[:], in_=hy[:], func=AF.Sigmoid, scale=1.5957691216),
                          reads=[hy.b()], writes=[hy.b()])
                    P.add("vector", lambda e, kv=kv, mc=mc: e.tensor_tensor(out=hid[:, kv, mc, :], in0=hy[:], in1=hx[:],
                                                                           op=ALU.mult),
                          reads=[hy.b(), hx.b()], writes=[hid.b()])
        for kv in range(2):
            for mc in range(2):
                P.add("tensor", lambda e, kv=kv, mc=mc: e.matmul(pX[0:64, kv * 64:(kv + 1) * 64], lhsT=w2c[:, kv, mc, :],
                                                                 rhs=hid[:, kv, mc, :], start=(mc == 0), stop=(mc == 1)),
                      reads=[w2c.b(), hid.b()], writes=[pX.b()])
        c0 = 32 * qt
        P.add("scalar", lambda e, c0=c0: e.copy(out=Kc[0:64, :, c0:c0 + 32],
                                                in_=pX[0:64, 0:64].rearrange("p (g n) -> p g n", g=2)),
              reads=[pX.b(), Kc.b("z")], writes=[Kc.b("k")])
        P.add("vector", lambda e, c0=c0: e.tensor_copy(out=vcT[:, :, c0:c0 + 32],
                                                       in_=pX[0:64, 64:128].rearrange("p (g n) -> p g n", g=2)),
              reads=[pX.b()], writes=[vcT.b()])
        for g in range(2):
            P.add("tensor", lambda e, g=g: e.transpose(out=pT[:, g, 0:64], in_=vcT[:, g, ctm * 128:(ctm + 1) * 128],
                                                       identity=identb[0:64, 0:64]),
                  reads=[vcT.b(), identb.b()], writes=[pT.b()])
        P.add("vector", lambda e: e.tensor_copy(out=vcs[:, ctm, :, :], in_=pT[:, 0:2, 0:64]),
              reads=[pT.b()], writes=[vcs.b()])
        stage(5)
        u = ub[0]
        uo = ub[0]
        if qt > 0:
            P.add("gpsimd", lambda e, u=u, uo=uo: e.tensor_copy(out=u[:, :, 0:30], in_=uo[:, :, 512:542]),
                  reads=[uo.b(c) for c in range(4)], writes=[u.b(c) for c in range(4)])
        def conv_chunk(c4):
            pa = pA[pa_ctr["i"] % 2]
            pa_ctr["i"] += 1
            for hf, (dgt, k0, nk) in enumerate(((dgbA, 0, 16), (dgbB, 16, 15))):
                P.add("vector", lambda e, dgt=dgt, k0=k0, nk=nk: e.tensor_tensor(
                    out=dgt[:], in0=identb[:].unsqueeze(1).to_broadcast([128, nk, 128]),
                    in1=dww[:, c4, k0:k0 + nk].unsqueeze(2).to_broadcast([128, nk, 128]), op=ALU.mult),
                    reads=[identb.b(), dww.b()], writes=[dgt.b()])
            for k in range(31):
                dgt, kk_ = (dgbA, k) if k < 16 else (dgbB, k - 16)
                P.add("tensor", lambda e, k=k, dgt=dgt, kk_=kk_: e.matmul(pa[:], lhsT=dgt[:, kk_, :], rhs=u[:, c4, k:k + 512],
                                                                        start=(k == 0), stop=(k == 30)),
                      reads=[dgt.b(), u.b(c4)], writes=[pa.b()])
            P.add("scalar", lambda e: e.activation(out=yb[:, c4, :], in_=pa[:], func=AF.Identity,
                                                   bias=dwb[:, c4:c4 + 1]),
                  reads=[pa.b(), dwb.b()], writes=[yb.b(c4)])

        def u_groups(c4):
            wt = next_w()
            pg = fm_group(wt, 0, 128)
            P.add("scalar", lambda e, pg=pg: e.activation(out=sg[:], in_=pg[:], func=AF.Sigmoid),
                  reads=[pg.b()], writes=[sg.b()])
            wt = next_w()
            pv = fm_group(wt, 0, 128)
            P.add("vector", lambda e, pv=pv, c4=c4, u=u: e.tensor_tensor(out=u[:, c4, 30:542], in0=pv[:], in1=sg[:],
                                                                        op=ALU.mult),
                  reads=[pv.b(), sg.b()], writes=[u.b(c4)])

        u_groups(0)
        u_groups(1)
        conv_chunk(0)
        u_groups(2)
        conv_chunk(1)
        u_groups(3)
        P.add("gpsimd", lambda e: e.tensor_scalar(out=Dq[64:128, :], in0=t0[64:128, :], scalar1=float(8 * qt),
                                                  scalar2=None, op0=ALU.add),
              reads=[t0.b()], writes=[Dq.b()])
        P.add("gpsimd", lambda e: e.tensor_scalar(out=NEGM[64:128, :], in0=Dq[64:128, :], scalar1=0.0, scalar2=-BIGM,
                                                  op0=ALU.is_lt, op1=ALU.mult),
              reads=[Dq.b()], writes=[NEGM.b()])
        P.add("gpsimd", lambda e, par=par: e.tensor_scalar(out=WBR[64:80, :], in0=drt[64:80, par, :], scalar1=8.0,
                                                           scalar2=-BIGM, op0=ALU.is_gt, op1=ALU.mult),
              reads=[drt.b()], writes=[WBR.b()])
        def q_proj(g):
            for hp in range(2):
                wt = next_w()
                for hh in range(2):
                    hl = hp * 2 + hh
                    h = 4 * g + hl
                    pa = fm_group(wt, hh * 64, 64)
                    P.add("scalar", lambda e, pa=pa, hl=hl: e.mul(out=Qc[0:64, hl, :], in_=pa[0:64, :], mul=0.125),
                          reads=[pa.b()], writes=[Qc.b(hl)])
                    P.add("gpsimd", lambda e, hl=hl: e.tensor_copy(out=Qw[0:64, hl, :], in_=Qc[0:64, hl, :]),
                          reads=[Qc.b(hl), Qw.b("z")], writes=[Qw.b(hl)])
                    P.add("gpsimd", lambda e, hl=hl: e.tensor_copy(out=Qs[0:64, hl, :], in_=Qc[0:64, hl, :]),
                          reads=[Qc.b(hl)], writes=[Qs.b(hl)])
                    P.add("vector", lambda e, hl=hl, h=h: e.scalar_tensor_tensor(
                        out=Qc[64:128, hl, :], in0=Dq[64:128, :], scalar=-64.0 * SLOPES[h], in1=NEGM[64:128, :],
                        op0=ALU.mult, op1=ALU.add), reads=[Dq.b(), NEGM.b()], writes=[Qc.b((hl, "b"))])
                    P.add("vector", lambda e, hl=hl, h=h, par=par: e.scalar_tensor_tensor(
                        out=Qw[64:80, hl, :], in0=drt[64:80, par, :], scalar=-64.0 * SLOPES[h], in1=WBR[64:80, :],
                        op0=ALU.mult, op1=ALU.add), reads=[drt.b(), WBR.b(), Qw.b("z")], writes=[Qw.b((hl, "b"))])
        def make_cmp(g):
            cstate = {}

            def cmp_front(hl):
                h = 4 * g + hl
                pcs = []
                for ct in range(ctm + 1):
                    psx = pS[ps_ctr["i"] % 2]
                    ps_ctr["i"] += 1
                    P.add("tensor", lambda e, ct=ct, psx=psx: e.matmul(
                        psx[:], lhsT=Kc[:, g, ct * 128:(ct + 1) * 128], rhs=Qc[:, hl, :], start=True, stop=True),
                        reads=[Kc.b("k"), Kc.b("oh"), Qc.b(hl), Qc.b((hl, "b"))], writes=[psx.b()])
                    pb = Pb[pb_ctr["i"] % 8]
                    pb_ctr["i"] += 1
                    P.add("scalar", lambda e, ct=ct, psx=psx, pb=pb: e.activation(
                        out=pb[:], in_=psx[:], func=AF.Exp, bias=biasc[:, ct, h:h + 1]),
                        reads=[psx.b(), biasc.b()], writes=[pb.b()])
                    if ct == ctm:
                        P.add("vector", lambda e, pb=pb: e.tensor_tensor(out=pb[:], in0=pb[:], in1=mcf[:, qt % 4, :],
                                                                        op=ALU.mult),
                              reads=[pb.b(), mcf.b()], writes=[pb.b()])
                    pcs.append(pb)
                cstate[hl] = pcs

            def cmp_back(hl):
                pcs = cstate.pop(hl)
                tA = (pX, pA[0])[hl % 2]
                tB = (pO[1], pO[0])[hl % 2]
                psA = tA[:, 0:260].rearrange("p (s d) -> p s d", s=4)
                psB = tB[:, 0:256].rearrange("p (s d) -> p s d", s=4)
                for s in range(4):
                    for ct in range(ctm + 1):
                        pb = pcs[ct]
                        P.add("tensor", lambda e, s=s, ct=ct, pb=pb: e.matmul(
                            psA[:, s, :], lhsT=pb[:, s * 128:(s + 1) * 128], rhs=ova[:, ct, :], start=(ct == 0),
                            stop=(ct == ctm)), reads=[pb.b(), ova.b()], writes=[tA.b()])
                        P.add("tensor", lambda e, s=s, ct=ct, pb=pb: e.matmul(
                            psB[:, s, :], lhsT=pb[:, s * 128:(s + 1) * 128], rhs=vcs[:, ct, g, :], start=(ct == 0),
                            stop=(ct == ctm)), reads=[pb.b(), vcs.b()], writes=[tB.b()])
                P.add("vector", lambda e: e.tensor_scalar(out=rden[:], in0=psA[:, :, 64], scalar1=1e-30,
                                                          scalar2=None, op0=ALU.max),
                      reads=[tA.b()], writes=[rden.b()])
                P.add("vector", lambda e: e.reciprocal(out=rden[:], in_=rden[:]), reads=[rden.b()], writes=[rden.b()])
                if hl == 0:
                    P.add("vector", lambda e: e.tensor_tensor(
                        out=impacc[:], in0=psA[:, :, 0:64], in1=rden[:, :].unsqueeze(2).to_broadcast([128, 4, 64]),
                        op=ALU.mult), reads=[tA.b(), rden.b()], writes=[impacc.b()])
                else:
                    P.add("vector", lambda e: e.tensor_tensor(
                        out=tmpc[:], in0=psA[:, :, 0:64], in1=rden[:, :].unsqueeze(2).to_broadcast([128, 4, 64]),
                        op=ALU.mult), reads=[tA.b(), rden.b()], writes=[tmpc.b()])
                    P.add("vector", lambda e: e.tensor_tensor(out=impacc[:], in0=impacc[:], in1=tmpc[:], op=ALU.add),
                          reads=[impacc.b(), tmpc.b()], writes=[impacc.b()])
                gi = (g * 4 + hl) * 3
                P.add("vector", lambda e: e.tensor_tensor(out=scg[:], in0=rden[:], in1=Gt[:, :, gi], op=ALU.mult),
                      reads=[rden.b()] + [Gt.b(s) for s in range(4)], writes=[scg.b()])
                P.add("vector", lambda e: e.tensor_tensor(
                    out=acc4[hl][:], in0=psB[:, :, :], in1=scg[:, :].unsqueeze(2).to_broadcast([128, 4, 64]),
                    op=ALU.mult), reads=[tB.b(), scg.b()], writes=[acc4[hl].b()])

            return cmp_front, cmp_back

        q_proj(0)
        cf0, cb0 = make_cmp(0)
        cf0(0)
        conv_chunk(2)
        cf0(1)
        cb0(0)
        conv_chunk(3)
        cf0(2)
        cb0(1)
        cf0(3)
        cb0(2)
        cb0(3)
        stage(6)
        for c4 in range(4):
            P.add("tensor", lambda e, c4=c4: e.matmul(pS[0][:], lhsT=onesf[:], rhs=yb[:, c4, :], start=(c4 == 0),
                                                      stop=(c4 == 3)),
                  reads=[onesf.b(), yb.b(c4)], writes=[pS[0].b()])
        for c4 in range(4):
            P.add("scalar", lambda e, c4=c4: e.activation(out=lnA[:], in_=yb[:, c4, :], func=AF.Square),
                  reads=[yb.b(c4)], writes=[lnA.b()])
            P.add("tensor", lambda e, c4=c4: e.matmul(pS[1][:], lhsT=onesf[:], rhs=lnA[:], start=(c4 == 0),
                                                      stop=(c4 == 3)),
                  reads=[onesf.b(), lnA.b()], writes=[pS[1].b()])
        P.add("scalar", lambda e: e.copy(out=lnB[:], in_=pS[0][:]), reads=[pS[0].b()], writes=[lnB.b()])
        P.add("scalar", lambda e: e.activation(out=lnA[:], in_=pS[0][:], func=AF.Square), reads=[pS[0].b()],
              writes=[lnA.b()])
        P.add("vector", lambda e: e.tensor_tensor(out=lnC[:], in0=pS[1][:], in1=lnA[:], op=ALU.subtract),
              reads=[pS[1].b(), lnA.b()], writes=[lnC.b()])
        P.add("scalar", lambda e: e.activation(out=lnC[:], in_=lnC[:], func=AF.Sqrt, bias=EPS), reads=[lnC.b()],
              writes=[lnC.b()])
        P.add("vector", lambda e: e.reciprocal(out=lnC[:], in_=lnC[:]), reads=[lnC.b()], writes=[lnC.b()])
        for c4 in range(4):
            P.add("vector", lambda e, c4=c4: e.tensor_tensor(out=lnA[:], in0=yb[:, c4, :], in1=lnB[:], op=ALU.subtract),
                  reads=[yb.b(c4), lnB.b()], writes=[lnA.b()])
            P.add("vector", lambda e: e.tensor_tensor(out=lnA[:], in0=lnA[:], in1=lnC[:], op=ALU.mult),
                  reads=[lnA.b(), lnC.b()], writes=[lnA.b()])
            P.add("scalar", lambda e, c4=c4: e.activation(out=AT[:, c4, :], in_=lnA[:], func=AF.Silu,
                                                          scale=clg[:, c4:c4 + 1], bias=clb[:, c4:c4 + 1]),
                  reads=[lnA.b(), clg.b(), clb.b()], writes=[AT.b(c4)])
        stage(7)
        for g in range(2):
            if g == 1:
                if qt + 1 < NT:
                    norm_stats(xt)
                q_proj(1)
            stage(8)
            if g == 1:
                cmp_front, cmp_back = make_cmp(1)
                cmp_front(0)
                for hl in range(4):
                    if hl + 1 < 4:
                        cmp_front(hl + 1)
                    cmp_back(hl)
            stage(9)

            def topk(s):
                jt0 = 8 * qt + 2 * s
                P.add("vector", lambda e: e.tensor_tensor(out=impm[:], in0=impacc[:, s, :],
                                                          in1=addw[:, 63 - jt0:127 - jt0], op=ALU.add),
                      reads=[impacc.b(), addw.b()], writes=[impm.b()])
                P.add("vector", lambda e: e.memset(impm[:, 0:1], 1e30), reads=[impm.b()], writes=[impm.b()])
                P.add("vector", lambda e: e.max(out=m8a[:], in_=impm[:]), reads=[impm.b()], writes=[m8a.b()])
                P.add("vector", lambda e: e.match_replace(out=imp2[:], in_to_replace=m8a[:], in_values=impm[:],
                                                          imm_value=-3e38),
                      reads=[impm.b(), m8a.b()], writes=[imp2.b()])
                P.add("vector", lambda e: e.max(out=m8b[:], in_=imp2[:]), reads=[imp2.b()], writes=[m8b.b()])
                P.add("vector", lambda e: e.tensor_scalar(out=selb[:, s, 64:128], in0=impm[:], scalar1=m8b[:, 7:8],
                                                          scalar2=BIGM, op0=ALU.is_ge, op1=ALU.mult),
                      reads=[impm.b(), m8b.b()], writes=[selb.b(s)])

            def selbias():
                for s in range(4):
                    P.add("tensor", lambda e, s=s: e.transpose(out=pT[:, s, :], in_=selb[:, s, :], identity=identb[:]),
                          reads=[selb.b(s), identb.b()], writes=[pT.b()])
                for hl in range(4):
                    P.add("vector", lambda e, hl=hl: e.scalar_tensor_tensor(
                        out=Qs[64:128, hl, :].rearrange("p (s q) -> p s q", s=4), in0=pT[64:128, 0:4, :], scalar=-BIGM,
                        in1=Qc[64:128, hl, :].rearrange("p (s q) -> p s q", s=4), op0=ALU.add, op1=ALU.add),
                        reads=[pT.b(), Qc.b((hl, "b"))], writes=[Qs.b((hl, "b"))])

            stage(10)
            tasks = []
            unit = 0
            for br in (1, 0):
                for hl in range(4):
                    if br == 0:
                        sl_h = SLOPES[4 * g + hl]
                        kts = [kt for kt in range(0, 4 * qt + 4) if sl_h * (q0 - (kt * 128 + 127)) < 160.0]
                        if len(kts) % 2:
                            kts = [kts[0] - 1] + kts
                    else:
                        kts = list(range(max(0, 4 * qt - 4), 4 * qt + 4))
                    for ki, kt in enumerate(kts):
                        tasks.append((hl, br, ki, kt, len(kts), unit))
                    unit += 1
            n_win = sum(1 for t in tasks if t[1] == 1)
            LAG = 2
            tstate = {}

            pair_ctr = {"i": 0}

            def front(pi):
                i0 = 2 * pi
                hl, br, _, _, n, un = tasks[i0]
                h = 4 * g + hl
                k2 = pair_ctr["i"] % 2
                pb2 = Pb2[pair_ctr["i"] % 4]
                pair_ctr["i"] += 1
                ptile, pviews = pPair[k2]
                Qx = Qs if br == 0 else Qw
                for half in range(2):
                    _, _, ki, kt, _, _ = tasks[i0 + half]
                    if br == 0:
                        lhs = Ks[:, g, kt * 128:(kt + 1) * 128]
                        kb = [Ks.b((g, kt // 4)), Ks.b("oh")]
                    else:
                        sl = kt % 8
                        lhs = Kw[:, g, sl * 128:(sl + 1) * 128]
                        kb = [Kw.b((g, (kt // 4) % 2)), Kw.b("oh")]
                    pv_ = pviews[half]
                    P.add("tensor", lambda e, lhs=lhs, pv_=pv_: e.matmul(pv_[:], lhsT=lhs, rhs=Qx[:, hl, :], start=True,
                                                                      stop=True),
                          reads=kb + [Qx.b(hl), Qx.b((hl, "b"))], writes=[pv_.b()])
                P.add("scalar", lambda e: e.activation(out=pb2[:], in_=ptile[:, 0:1024], func=AF.Exp,
                                                       bias=biask[:, h:h + 1]),
                      reads=[pviews[0].b(), pviews[1].b(), biask.b()], writes=[pb2.b(0), pb2.b(1)])
                for half in range(2):
                    _, _, ki, kt, _, _ = tasks[i0 + half]
                    if kt >= 4 * qt:
                        m, tri = kt - 4 * qt, trid
                    elif br == 1:
                        m, tri = kt - (4 * qt - 4), trif
                    else:
                        continue
                    for hh_ in range(2):
                        c0_ = half * 512 + (2 * m + hh_) * 64
                        blk = pb2[hh_ * 64:(hh_ + 1) * 64, c0_:c0_ + 64]
                        trv = tri[hh_ * 64:(hh_ + 1) * 64, :]
                        P.add("vector", lambda e, blk=blk, trv=trv: e.tensor_tensor(out=blk, in0=blk, in1=trv, op=ALU.mult),
                              reads=[pb2.b(half), tri.b()], writes=[pb2.b(half)])
                tstate[pi] = pb2

            def back(pj):
                pb2 = tstate.pop(pj)
                for half in range(2):
                    hl, br, ki, kt, n, un = tasks[2 * pj + half]
                    po = pO[un % 2]
                    if br == 0:
                        vap = Vs[:, kt, g, :]
                        vb = Vs.b(kt)
                    else:
                        vap = Vw[:, kt % 8, g, :]
                        vb = Vw.b(kt % 8)
                    pbh = pb2[:, half * 512:(half + 1) * 512]
                    P.add("tensor", lambda e, vap=vap, pbh=pbh, po=po, ki=ki, n=n: e.matmul(
                        po[0:65, :], lhsT=vap, rhs=pbh, start=(ki == 0), stop=(ki == n - 1)),
                        reads=[vb, pb2.b(half)], writes=[po.b()])
                    if ki == n - 1:
                        ot = OT[un % 2]
                        P.add("vector", lambda e, ot=ot, po=po: e.tensor_copy(out=ot[:], in_=po[0:65, :]),
                              reads=[po.b()], writes=[ot.b()])

            def epilogue(hl, br, un):
                h = 4 * g + hl
                ot = OT[un % 2]
                psO = pX[:, 0:260].rearrange("p (s d) -> p s d", s=4)
                for s in range(4):
                    P.add("tensor", lambda e, s=s: e.transpose(out=psO[:, s, :], in_=ot[:, s * 128:(s + 1) * 128],
                                                               identity=identf[0:65, 0:65]),
                          reads=[ot.b(), identf.b()], writes=[pX.b()])
                P.add("vector", lambda e: e.reciprocal(out=rden[:], in_=psO[:, :, 64]), reads=[pX.b()], writes=[rden.b()])
                gi = (g * 4 + hl) * 3 + 1 + br
                P.add("vector", lambda e: e.tensor_tensor(out=scg[:], in0=rden[:], in1=Gt[:, :, gi], op=ALU.mult),
                      reads=[rden.b()] + [Gt.b(s) for s in range(4)], writes=[scg.b()])
                P.add("vector", lambda e: e.tensor_tensor(
                    out=tmpc[:], in0=psO[:, :, 0:64], in1=scg[:, :].unsqueeze(2).to_broadcast([128, 4, 64]),
                    op=ALU.mult), reads=[pX.b(), scg.b()], writes=[tmpc.b()])
                if br == 1:
                    P.add("vector", lambda e: e.tensor_tensor(out=acc4[hl][:], in0=acc4[hl][:], in1=tmpc[:], op=ALU.add),
                          reads=[acc4[hl].b(), tmpc.b()], writes=[acc4[hl].b()])
                else:
                    P.add("vector", lambda e: e.tensor_tensor(out=atok[:, :, h * 64:(h + 1) * 64], in0=acc4[hl][:],
                                                              in1=tmpc[:], op=ALU.add),
                          reads=[acc4[hl].b(), tmpc.b()], writes=[atok.b(h)])

            nT = len(tasks)
            assert nT % 2 == 0 and n_win % 2 == 0
            nP = nT // 2
            LAGP = 2
            pend = {}
            for i in range(nP + LAGP + 4):
                if i in (1, 2, 3, 4):
                    topk(i - 1)
                if i == n_win // 2:
                    selbias()
                if i < nP:
                    front(i)
                j = i - LAGP
                if 0 <= j < nP:
                    back(j)
                    t_last = tasks[2 * j + 1]
                    if t_last[2] == t_last[4] - 1:
                        pend.setdefault(i + 2, []).append((t_last[0], t_last[1], t_last[5]))
                for (hl_, br_, un_) in pend.pop(i, []):
                    epilogue(hl_, br_, un_)
            assert not pend and not tstate
        stage(11)
        for s in range(4):
            for c4 in range(4):
                P.add("tensor", lambda e, s=s, c4=c4: e.transpose(out=pT[:, c4, :], in_=atok[:, s, c4 * 128:(c4 + 1) * 128],
                                                                  identity=identb[:]),
                      reads=[atok.b(h) for h in range(8)] + [identb.b()], writes=[pT.b()])
            P.add("scalar", lambda e, s=s: e.copy(out=AT[:, 4:8, s * 128:(s + 1) * 128], in_=pT[:, 0:4, :]),
                  reads=[pT.b()], writes=[AT.b((4, s))])
        dbg("AT%d" % qt, AT[:], [128, 8, 512], [AT.b(c) for c in range(4)] + [AT.b((4, s)) for s in range(4)])
        stage(12)
        atb = [AT.b(c) for c in range(4)] + [AT.b((4, s)) for s in range(4)]
        P.add("sync", lambda e: e.dma_start(out=xs[:], in_=xv[qt][:, 0, :]), writes=[xs.b()], dma="xs")
        for s in range(4):
            for nh in range(2):
                pa = pA[pa_ctr["i"] % 2]
                pa_ctr["i"] += 1
                for kc in range(8):
                    P.add("tensor", lambda e, s=s, nh=nh, kc=kc, pa=pa: e.matmul(
                        pa[:], lhsT=AT[:, kc, s * 128:(s + 1) * 128], rhs=wout[:, kc, nh * 512:(nh + 1) * 512],
                        start=(kc == 0), stop=(kc == 7)), reads=atb + [wout.b()], writes=[pa.b()])
                P.add("vector", lambda e, s=s, nh=nh, pa=pa: e.tensor_tensor(
                    out=xs[:, nh * 512:(nh + 1) * 512], in0=pa[:], in1=xs[:, nh * 512:(nh + 1) * 512], op=ALU.add),
                    reads=[pa.b(), xs.b()], writes=[xs.b()])
            P.add("sync", lambda e, s=s: e.dma_start(out=x1v[qt][:, s, :], in_=xs[:]), reads=[xs.b()], dma="x1st")
            if s + 1 < 4:
                P.add("sync", lambda e, s=s: e.dma_start(out=xs[:], in_=xv[qt][:, s + 1, :]), writes=[xs.b()], dma="xs")
      except StopStage:
        pass

    P.barrier()
    A.close()
    B = ExitStack()
    P.stack = B
    w1sb = P.sb("w1sb", [128, 8, DFF], BF16)
    w2sb = P.sb("w2sb", [128, 32, D], BF16)
    fT = P.sb("fT", [128, 32, 512], BF16)
    gfb = P.sb("gfb", [128, D], F32)
    ld("sync", gfb[:], gf.partition_broadcast(128), [gfb.b()], "c_gfb")
    rl = [P.sb("rl%d" % i, [128, 512], F32) for i in range(1)]
    w1v = w_ff1.rearrange("(kc p) n -> p kc n", p=128)
    w2v = w_ff2.rearrange("(kc p) n -> p kc n", p=128)
    for kc in range(8):
        P.add("gpsimd", lambda e, kc=kc: e.dma_start(out=w1sb[:, kc, :], in_=w1v[:, kc, :]),
              writes=[w1sb.b(kc)], dma="w1_%d" % kc)
    for kc in range(32):
        P.add("gpsimd", lambda e, kc=kc: e.dma_start(out=w2sb[:, kc, :], in_=w2v[:, kc, :]),
              writes=[w2sb.b(kc % 8)], dma="w2_%d" % (kc % 8))
    ld("sync", gbt[:], g2.partition_broadcast(128), [gbt.b()], "c_gbt2")
    ssq2 = P.sb("ssq2", [128, 4], F32)
    rstd2 = P.sb("rstd2", [128, 4], F32)
    xsb = [P.sb("xsb%d" % i, [128, D], F32) for i in range(2)]
    for s_ in range(4):
        load_x(x1v, 0, s_, "xtb")
    norm_stats(xt)
    for qt in range(NT):
        norm_apply(xt, gbt)
        if qt + 1 < NT:
            for s_ in range(4):
                load_x(x1v, qt + 1, s_, "xtb")
        for n in range(32):
            pa = pA[n % 2]
            for kc in range(8):
                P.add("tensor", lambda e, kc=kc, n=n, pa=pa: e.matmul(pa[:], lhsT=w1sb[:, kc, n * 128:(n + 1) * 128],
                                                                     rhs=hT[:, kc, :], start=(kc == 0), stop=(kc == 7)),
                      reads=hTall + [w1sb.b(kc)], writes=[pa.b()])
            r = rl[0]
            P.add("scalar", lambda e, r=r, pa=pa: e.activation(out=r[:], in_=pa[:], func=AF.Relu),
                  reads=[pa.b()], writes=[r.b()])
            P.add("vector", lambda e, r=r, n=n: e.tensor_tensor(out=fT[:, n, :], in0=r[:], in1=r[:], op=ALU.mult),
                  reads=[r.b()], writes=[fT.b(n)])
        if qt + 1 < NT:
            norm_stats(xt)
        fall = [fT.b(n) for n in range(32)]
        for s in range(4):
            xc = xsb[s % 2]
            P.add("sync", lambda e, s=s, xc=xc: e.dma_start(out=xc[:], in_=x1v[qt][:, s, :]), writes=[xc.b()],
                  dma="xsb%d" % (s % 2))
            for nh in range(2):
                pa = pS[nh]
                for n in range(32):
                    P.add("tensor", lambda e, n=n, s=s, nh=nh, pa=pa: e.matmul(
                        pa[:], lhsT=fT[:, n, s * 128:(s + 1) * 128], rhs=w2sb[:, n, nh * 512:(nh + 1) * 512],
                        start=(n == 0), stop=(n == 31)),
                        reads=fall + [w2sb.b(n % 8)], writes=[pa.b()])
                P.add("vector", lambda e, nh=nh, pa=pa, xc=xc: e.tensor_tensor(
                    out=xc[:, nh * 512:(nh + 1) * 512], in0=pa[:], in1=xc[:, nh * 512:(nh + 1) * 512], op=ALU.add),
                    reads=[pa.b(), xc.b()], writes=[xc.b()])
            P.add("scalar", lambda e, s=s, xc=xc: e.activation(out=junk[:], in_=xc[:], func=AF.Square,
                                                               accum_out=ssq2[:, s:s + 1]),
                  reads=[xc.b()], writes=[junk.b(), ssq2.b(s)])
            P.add("scalar", lambda e, s=s: e.activation(out=rstd2[:, s:s + 1], in_=ssq2[:, s:s + 1], func=AF.Sqrt,
                                                        scale=1.0 / D, bias=EPS),
                  reads=[ssq2.b(s)], writes=[rstd2.b(s)])
            P.add("vector", lambda e, s=s: e.reciprocal(out=rstd2[:, s:s + 1], in_=rstd2[:, s:s + 1]),
                  reads=[rstd2.b(s)], writes=[rstd2.b(s)])
            P.add("vector", lambda e, s=s, xc=xc: e.scalar_tensor_tensor(out=xc[:], in0=xc[:], scalar=rstd2[:, s:s + 1],
                                                                        in1=gfb[:], op0=ALU.mult, op1=ALU.mult),
                  reads=[xc.b(), rstd2.b(s), gfb.b()], writes=[xc.b()])
            P.add("sync", lambda e, s=s, xc=xc: e.dma_start(out=outv[qt][:, s, :], in_=xc[:]), reads=[xc.b()],
                  dma="ost%d" % (s % 2))
    P.stack = outer_stack
    P.emit()
    P._phase_stacks = (A, B)
    return nc, P, dbg_out


_CACHE = {}


def _get_prog(T):
    if T not in _CACHE:
        _CACHE[T] = build(T)
    return _CACHE[T]


def make_inputs_for_core(xb, wl, consts, T):
    m = dict(wl)
    m.update(consts)
    m["x"] = np.ascontiguousarray(xb, dtype=np.float32)
    ohk = np.zeros((64, T), np.float32)
    for j in range(T // 64):
        ohk[j, j * 64:(j + 1) * 64] = 1
    m["ohk"] = ohk
    return m


def kernel(**inputs):
    x = np.asarray(inputs["x"], np.float32)
    Bn, T, _ = x.shape
    wl = layout_weights(inputs)
    consts = make_consts()
    nc, P, _ = _get_prog(T)
    in_maps = [make_inputs_for_core(x[b], wl, consts, T) for b in range(Bn)]
    res = run_bass_kernel_spmd(nc, in_maps, core_ids=list(range(Bn)))
    return np.stack([np.asarray(r["out"], np.float32) for r in res.results], axis=0)
```
